# Optimizing a Trainium2 kernel written in Bass

```python
import math
import jax
import jax.numpy as jnp
from jax import lax
import numpy as np

D_MODEL = 2048
BATCH = 2
SEQ = 4096
DEPTH = 4

N_MIXERS = 3
HEAD_DIM = 128
N_HEADS = D_MODEL // HEAD_DIM
N_KV_HEADS = 4
GQA_GROUP = N_HEADS // N_KV_HEADS
ATTN_WIDTH = N_HEADS * HEAD_DIM
KV_WIDTH = N_KV_HEADS * HEAD_DIM
WINDOW_A = 128
BLOCK_A = 128
DILATED_PATTERNS = ((128, 1), (512, 4), (2048, 16))
B_GROUP_HEADS = tuple(N_HEADS // 3 + (1 if g < N_HEADS % 3 else 0) for g in range(3))
BLOCK_B = 64
DIFF_HEAD_DIM = HEAD_DIM // 2
Q_BLOCK_C = 128
FFN_HIDDEN = -(-8 * D_MODEL // (3 * 256)) * 256
RMS_EPS = 1e-6
NEG_INF = -1e30

kernel_name = "hybrid_interleaved_encoder"


def _n_layers_with(m):
    return len(range(m, DEPTH, N_MIXERS))


def _rmsnorm(x, g):
    xf = x.astype(jnp.float32)
    y = xf * lax.rsqrt(jnp.mean(xf * xf, axis=-1, keepdims=True) + RMS_EPS)
    return (y * g.astype(jnp.float32)).astype(x.dtype)


def _alibi_slopes(n):
    return jnp.exp2(-8.0 * jnp.arange(1, n + 1, dtype=jnp.float32) / n)


def _banded_attention(q, k, v, slopes, block, half_window, dist_scale, sink=None):
    bt, L, kvh, grp, hd = q.shape
    nb = -(-L // block)
    lp = nb * block
    extra = lp - L
    q = jnp.pad(q, ((0, 0), (0, extra), (0, 0), (0, 0), (0, 0)))
    kp = jnp.pad(k, ((0, 0), (half_window, extra + half_window), (0, 0), (0, 0)))
    vp = jnp.pad(v, ((0, 0), (half_window, extra + half_window), (0, 0), (0, 0)))
    span = block + 2 * half_window
    idx = jnp.arange(nb)[:, None] * block + jnp.arange(span)[None, :]
    kb = kp[:, idx]
    vb = vp[:, idx]
    qb = q.reshape(bt, nb, block, kvh, grp, hd)
    logits = jnp.einsum('bnqhgd,bnkhd->bnhgqk', qb, kb).astype(jnp.float32) / math.sqrt(hd)
    rel = jnp.arange(block)[:, None] - jnp.arange(span)[None, :] + half_window
    kpos = idx - half_window
    valid = (jnp.abs(rel) <= half_window)[None] & ((kpos >= 0) & (kpos < L))[:, None, :]
    dist = (jnp.abs(rel) * dist_scale).astype(jnp.float32)
    logits = logits - slopes[:, :, None, None] * dist
    logits = jnp.where(valid[None, :, None, None], logits, NEG_INF)
    mx = jnp.max(logits, axis=-1)
    if sink is not None:
        mx = jnp.maximum(mx, sink[None, None, :, :, None])
    e = jnp.exp(logits - mx[..., None])
    denom = jnp.sum(e, axis=-1)
    if sink is not None:
        denom = denom + jnp.exp(sink[None, None, :, :, None] - mx)
    p = e / denom[..., None]
    out = jnp.einsum('bnhgqk,bnkhd->bnqhgd', p.astype(v.dtype), vb)
    out = out.reshape(bt, lp, kvh, grp, hd)[:, :L]
    lse = (mx + jnp.log(denom)).transpose(0, 1, 4, 2, 3).reshape(bt, lp, kvh, grp)[:, :L]
    return out, lse


def _mixer_window_gqa(h, w_qkv, w_o, sink):
    b, s, _ = h.shape
    q, k, v = jnp.split(h @ w_qkv, [ATTN_WIDTH, ATTN_WIDTH + KV_WIDTH], axis=-1)
    q = q.reshape(b, s, N_KV_HEADS, GQA_GROUP, HEAD_DIM)
    k = k.reshape(b, s, N_KV_HEADS, HEAD_DIM)
    v = v.reshape(b, s, N_KV_HEADS, HEAD_DIM)
    slopes = _alibi_slopes(N_HEADS).reshape(N_KV_HEADS, GQA_GROUP)
    sink_hg = sink.astype(jnp.float32).reshape(N_KV_HEADS, GQA_GROUP)
    out, _ = _banded_attention(q, k, v, slopes, BLOCK_A, WINDOW_A, 1, sink_hg)
    return out.reshape(b, s, ATTN_WIDTH) @ w_o


def _to_strided(t, dil):
    b, s, hg, hd = t.shape
    return t.reshape(b, s // dil, dil, hg, hd).transpose(0, 2, 1, 3, 4).reshape(b * dil, s // dil, hg, hd)


def _from_strided(t, b, dil):
    rest = t.shape[2:]
    t = t.reshape((b, dil) + t.shape[1:])
    perm = (0, 2, 1) + tuple(range(3, t.ndim))
    return t.transpose(perm).reshape((b, t.shape[2] * dil) + rest)


def _mixer_dilated(h, w_qkv, w_o):
    b, s, _ = h.shape
    q, k, v = (t.reshape(b, s, N_HEADS, HEAD_DIM) for t in jnp.split(h @ w_qkv, 3, axis=-1))
    slopes = _alibi_slopes(N_HEADS)
    outs, scores = [], []
    h0 = 0
    for (window, dil), hg in zip(DILATED_PATTERNS, B_GROUP_HEADS):
        half = window // (2 * dil)
        sl = slice(h0, h0 + hg)
        o, lse = _banded_attention(_to_strided(q[:, :, sl], dil)[:, :, :, None],
                                   _to_strided(k[:, :, sl], dil), _to_strided(v[:, :, sl], dil),
                                   slopes[sl][:, None], BLOCK_B, half, dil)
        o = _from_strided(o[:, :, :, 0], b, dil).reshape(b, s, hg * HEAD_DIM)
        lse = _from_strided(lse[..., 0], b, dil)
        outs.append(o)
        scores.append(jnp.mean(lse, axis=-1))
        h0 += hg
    alpha = jax.nn.softmax(jnp.stack(scores, axis=-1), axis=-1)
    mixed = jnp.concatenate([o * alpha[..., g:g + 1].astype(o.dtype) for g, o in enumerate(outs)], axis=-1)
    return mixed @ w_o


def _mixer_diff(h, w_qkv, w_o, lq1, lk1, lq2, lk2, subln_g, lambda_init):
    b, s, _ = h.shape
    q, k, v = jnp.split(h @ w_qkv, [ATTN_WIDTH, ATTN_WIDTH + KV_WIDTH], axis=-1)
    q = q.reshape(b, s, N_KV_HEADS, GQA_GROUP, 2, DIFF_HEAD_DIM)
    k = k.reshape(b, s, N_KV_HEADS, 2, DIFF_HEAD_DIM)
    v = v.reshape(b, s, N_KV_HEADS, 2 * DIFF_HEAD_DIM)
    f32 = jnp.float32
    lam = (jnp.exp(jnp.sum(lq1.astype(f32) * lk1.astype(f32)))
           - jnp.exp(jnp.sum(lq2.astype(f32) * lk2.astype(f32))) + lambda_init)
    slopes = _alibi_slopes(N_HEADS).reshape(N_KV_HEADS, GQA_GROUP)
    scale = DIFF_HEAD_DIM ** -0.5
    nb = s // Q_BLOCK_C
    kpos = jnp.arange(s)
    qblocks = q.reshape(b, nb, Q_BLOCK_C, N_KV_HEADS, GQA_GROUP, 2, DIFF_HEAD_DIM).transpose(1, 0, 2, 3, 4, 5, 6)

    def block(args):
        qb, j = args
        logits = jnp.einsum('bqhgcd,bkhcd->bhgcqk', qb, k).astype(f32) * scale
        qpos = j * Q_BLOCK_C + jnp.arange(Q_BLOCK_C)
        dist = jnp.abs(qpos[:, None] - kpos[None, :]).astype(f32)
        logits = logits - slopes[:, :, None, None, None] * dist
        p = jax.nn.softmax(logits, axis=-1)
        attn = p[:, :, :, 0] - lam * p[:, :, :, 1]
        return jnp.einsum('bhgqk,bkhd->bqhgd', attn.astype(v.dtype), v)

    o = lax.map(block, (qblocks, jnp.arange(nb)))
    o = o.transpose(1, 0, 2, 3, 4, 5).reshape(b, s, N_HEADS, 2 * DIFF_HEAD_DIM)
    o = _rmsnorm(o, subln_g) * (1.0 - lambda_init)
    return o.reshape(b, s, ATTN_WIDTH) @ w_o


def _swiglu(h, w_gate, w_up, w_down):
    return (jax.nn.silu(h @ w_gate) * (h @ w_up)) @ w_down


def setup_inputs(seed: int = 0) -> dict:
    key = jax.random.key(seed)
    ks = jax.random.split(key, 20)
    f32 = jnp.float32
    n_a, n_b, n_c = (_n_layers_with(m) for m in range(N_MIXERS))

    def w(k, shape, fan_in):
        return jax.random.normal(k, shape, f32) * fan_in ** -0.5

    def gain(k, shape):
        return 1.0 + 0.1 * jax.random.normal(k, shape, f32)

    return {
        "x": jax.random.normal(ks[0], (BATCH, SEQ, D_MODEL), f32),
        "mix_pre_g": gain(ks[1], (DEPTH, D_MODEL)),
        "mix_post_g": gain(ks[2], (DEPTH, D_MODEL)),
        "ffn_pre_g": gain(ks[3], (DEPTH, D_MODEL)),
        "ffn_post_g": gain(ks[4], (DEPTH, D_MODEL)),
        "w_gate": w(ks[5], (DEPTH, D_MODEL, FFN_HIDDEN), D_MODEL),
        "w_up": w(ks[6], (DEPTH, D_MODEL, FFN_HIDDEN), D_MODEL),
        "w_down": w(ks[7], (DEPTH, FFN_HIDDEN, D_MODEL), FFN_HIDDEN),
        "a_w_qkv": w(ks[8], (n_a, D_MODEL, ATTN_WIDTH + 2 * KV_WIDTH), D_MODEL),
        "a_w_o": w(ks[9], (n_a, ATTN_WIDTH, D_MODEL), ATTN_WIDTH),
        "a_sink": jax.random.normal(ks[10], (n_a, N_HEADS), f32),
        "b_w_qkv": w(ks[11], (n_b, D_MODEL, 3 * ATTN_WIDTH), D_MODEL),
        "b_w_o": w(ks[12], (n_b, ATTN_WIDTH, D_MODEL), ATTN_WIDTH),
        "c_w_qkv": w(ks[13], (n_c, D_MODEL, ATTN_WIDTH + 2 * KV_WIDTH), D_MODEL),
        "c_w_o": w(ks[14], (n_c, ATTN_WIDTH, D_MODEL), ATTN_WIDTH),
        "c_lambda_q1": 0.1 * jax.random.normal(ks[15], (n_c, DIFF_HEAD_DIM), f32),
        "c_lambda_k1": 0.1 * jax.random.normal(ks[16], (n_c, DIFF_HEAD_DIM), f32),
        "c_lambda_q2": 0.1 * jax.random.normal(ks[17], (n_c, DIFF_HEAD_DIM), f32),
        "c_lambda_k2": 0.1 * jax.random.normal(ks[18], (n_c, DIFF_HEAD_DIM), f32),
        "c_subln_g": gain(ks[19], (n_c, 2 * DIFF_HEAD_DIM)),
    }


def reference(x, mix_pre_g, mix_post_g, ffn_pre_g, ffn_post_g, w_gate, w_up, w_down,
              a_w_qkv, a_w_o, a_sink, b_w_qkv, b_w_o, c_w_qkv, c_w_o,
              c_lambda_q1, c_lambda_k1, c_lambda_q2, c_lambda_k2, c_subln_g):
    for i in range(DEPTH):
        m, j = i % N_MIXERS, i // N_MIXERS
        h = _rmsnorm(x, mix_pre_g[i])
        if m == 0:
            y = _mixer_window_gqa(h, a_w_qkv[j], a_w_o[j], a_sink[j])
        elif m == 1:
            y = _mixer_dilated(h, b_w_qkv[j], b_w_o[j])
        else:
            y = _mixer_diff(h, c_w_qkv[j], c_w_o[j], c_lambda_q1[j], c_lambda_k1[j],
                            c_lambda_q2[j], c_lambda_k2[j], c_subln_g[j],
                            0.8 - 0.6 * math.exp(-0.3 * i))
        x = x + _rmsnorm(y, mix_post_g[i])
        h = _rmsnorm(x, ffn_pre_g[i])
        x = x + _rmsnorm(_swiglu(h, w_gate[i], w_up[i], w_down[i]), ffn_post_g[i])
    return x
```

```python
import numpy as np
import concourse.bass as bass
import concourse.mybir as mybir
from concourse.bass_utils import run_bass_kernel_spmd

F32 = mybir.dt.float32
BF16 = mybir.dt.bfloat16
I32 = mybir.dt.int32
AF = mybir.ActivationFunctionType
ALU = mybir.AluOpType
ESZ = {F32: 4, BF16: 2, I32: 4}
CELL = 512


class Ref:
    __slots__ = ("ap", "keys")

    def __init__(self, ap, keys):
        self.ap = ap
        self.keys = keys


class SB:
    def __init__(self, nc, name, off, A, B, dt, base):
        self.A, self.B, self.dt, self.off = A, B, dt, off
        self.esz = ESZ[dt]
        self.h = nc.alloc_sbuf_tensor_at(name, [128, A, B], dt, offset=base + off)
        self.nbytes = A * B * self.esz

    def r(self, a=0, b0=0, b1=None, p0=0, p1=128, a1=None):
        if b1 is None:
            b1 = self.B
        if a1 is None:
            ap = self.h[p0:p1, a, b0:b1]
            lo = self.off + (a * self.B + b0) * self.esz
            hi = self.off + (a * self.B + b1) * self.esz
        else:
            ap = self.h[p0:p1, a:a1, b0:b1]
            lo = self.off + (a * self.B + b0) * self.esz
            hi = self.off + ((a1 - 1) * self.B + b1) * self.esz
        return Ref(ap, [("sb", c) for c in range(lo // CELL, (hi - 1) // CELL + 1)])


class PS:
    def __init__(self, nc, name, idx):
        self.h = nc.alloc_psum_tensor(name, [128, 512], F32)
        self.idx = idx

    def r(self, b0=0, b1=512, p0=0, p1=128):
        lo, hi = b0 * 4, b1 * 4
        return Ref(self.h[p0:p1, b0:b1], [("ps", self.idx, c) for c in range(lo // CELL, (hi - 1) // CELL + 1)])


def dref(ap, name):
    return Ref(ap, [("dram", name)])


COMPUTE = ("pe", "act", "dve", "pool")


class Prog:
    def __init__(self, nc, n_dma_sems=12):
        self.nc = nc
        self.streams = {k: [] for k in ("pe", "act", "dve", "pool", "sp")}
        self.last_w = {}
        self.readers = {}
        self.n_dma_sems = n_dma_sems
        self.dma_rr = {"sp": 0, "pool": 0, "act": 0, "cc": 0}
        self.dma_cnt = {}
        self.dma_inc = {}
        self.flag = set()
        self.order = []

    def add(self, eng, fn, reads=(), writes=(), dma=None, inc=16, extra_deps=(), semq=None):
        deps = {}

        def need(tok):
            if tok is None:
                return
            s, i = tok
            if s == eng and eng == "pe" and dma is None:
                return
            if deps.get(s, -1) < i:
                deps[s] = i

        for r in reads:
            for k in r.keys:
                need(self.last_w.get(k))
        for w in writes:
            for k in w.keys:
                need(self.last_w.get(k))
                rd = self.readers.get(k)
                if rd:
                    for s, i in rd.items():
                        need((s, i))
        for t in extra_deps:
            need(t)
        if dma is not None:
            q = semq or eng
            j = self.dma_rr[q]
            self.dma_rr[q] = (j + 1) % (2 if q == "cc" else self.n_dma_sems)
            sid = ("dma", q, j)
            n = self.dma_cnt.get(sid, 0) + 1
            self.dma_cnt[sid] = n
            self.dma_inc[sid] = inc
            if n > 1:
                need((sid, n - 1))
            tok = (sid, n)
            self.streams[eng].append(dict(fn=fn, deps=deps, dma=sid, inc=inc))
        else:
            idx = len(self.streams[eng])
            tok = (eng, idx)
            self.streams[eng].append(dict(fn=fn, deps=deps, dma=None))
        for s, i in deps.items():
            if s in COMPUTE:
                self.flag.add((s, i))
        for r in reads:
            for k in r.keys:
                self.readers.setdefault(k, {})[tok[0]] = tok[1]
        for w in writes:
            for k in w.keys:
                self.last_w[k] = tok
                self.readers[k] = {}
        return tok

    def emit(self, final_wait_tokens):
        nc = self.nc
        sems = {}
        import contextlib
        with contextlib.ExitStack() as st:
            for e in COMPUTE:
                sems[e] = st.enter_context(nc.semaphore("s_" + e))
            for sid in self.dma_cnt:
                sems[sid] = st.enter_context(nc.semaphore("d_%s_%d" % (sid[1], sid[2])))
            block = st.enter_context(nc.Block())
            cum = {}
            for e in COMPUTE:
                c = 0
                arr = []
                for i in range(len(self.streams[e])):
                    if (e, i) in self.flag:
                        c += 1
                    arr.append(c)
                cum[e] = arr

            def val(s, i):
                if s in COMPUTE:
                    return cum[s][i]
                return self.dma_inc[s] * i

            def run(ename, eng):
                waited = {}
                for i, ins in enumerate(self.streams[ename]):
                    for s, di in ins["deps"].items():
                        v = val(s, di)
                        if waited.get(s, 0) < v:
                            eng.wait_ge(sems[s], v)
                            waited[s] = v
                    r = ins["fn"](eng)
                    if ins["dma"] is not None:
                        r.then_inc(sems[ins["dma"]], ins["inc"])
                    elif (ename, i) in self.flag:
                        r.then_inc(sems[ename], 1)
                if ename == "sp":
                    for s, di in final_wait_tokens:
                        v = val(s, di)
                        if waited.get(s, 0) < v:
                            eng.wait_ge(sems[s], v)
                            waited[s] = v

            @block.tensor
            def _(e):
                run("pe", e)

            @block.scalar
            def _(e):
                run("act", e)

            @block.vector
            def _(e):
                run("dve", e)

            @block.gpsimd
            def _(e):
                run("pool", e)

            @block.sync
            def _(e):
                run("sp", e)

    def getpid(self, e):
        if not hasattr(self, "_pids"):
            self._pids = {}
        k = id(e)
        if k not in self._pids:
            self._pids[k] = e.partition_id()
        return self._pids[k]

    def mm(self, out, lhsT, rhs, start=True, stop=True):
        rd = [lhsT, rhs] + ([] if start else [out])
        return self.add("pe", lambda e: e.matmul(out.ap, lhsT.ap, rhs.ap, start=start, stop=stop), rd, [out])

    def act(self, out, in_, func, bias=None, scale=1.0):
        rd = [in_] + ([bias] if isinstance(bias, Ref) else [])
        if bias is None:
            return self.add("act", lambda e: e.activation(out.ap, in_.ap, func, scale=scale), rd, [out])
        b = bias.ap if isinstance(bias, Ref) else bias
        return self.add("act", lambda e: e.activation(out.ap, in_.ap, func, bias=b, scale=scale), rd, [out])

    def tt(self, out, a, b, op, eng="dve"):
        return self.add(eng, lambda e: e.tensor_tensor(out.ap, a.ap, b.ap, op), [a, b], [out])

    def stt(self, out, in0, scalar, in1, op0, op1, eng="dve"):
        rd = [in0, in1] + ([scalar] if isinstance(scalar, Ref) else [])
        s = scalar.ap if isinstance(scalar, Ref) else scalar
        return self.add(eng, lambda e: e.scalar_tensor_tensor(out.ap, in0.ap, s, in1.ap, op0, op1), rd, [out])

    def ts(self, out, in0, s1, s2, op0, op1=None, eng="dve"):
        rd = [in0] + [x for x in (s1, s2) if isinstance(x, Ref)]
        a1 = s1.ap if isinstance(s1, Ref) else s1
        a2 = s2.ap if isinstance(s2, Ref) else s2
        if op1 is None:
            return self.add(eng, lambda e: e.tensor_scalar(out.ap, in0.ap, a1, None, op0), rd, [out])
        return self.add(eng, lambda e: e.tensor_scalar(out.ap, in0.ap, a1, a2, op0, op1), rd, [out])

    def copy(self, out, in_, eng="dve"):
        return self.add(eng, lambda e: e.tensor_copy(out.ap, in_.ap), [in_], [out])

    def recip(self, out, in_):
        return self.add("dve", lambda e: e.reciprocal(out.ap, in_.ap), [in_], [out])

    def dma(self, q, out, in_):
        def fn(e):
            o = out.ap(e) if callable(out.ap) else out.ap
            i = in_.ap(e) if callable(in_.ap) else in_.ap
            return e.dma_start(out=o, in_=i)
        return self.add(q, fn, [in_], [out], dma=True)


import math
import ml_dtypes

D = 2048
T = 1024
NH = 16
FF = 5632
NHC = FF // 128
EPS = 1e-6
BIG = 1.0e9
NEGV = -30000.0


def slopes16():
    return [2.0 ** (-8.0 * (h + 1) / 16.0) for h in range(16)]


def b_group(h):
    return 0 if h < 6 else (1 if h < 11 else 2)


B_OFFS = {0: list(range(-1, 2)), 1: list(range(-2, 3)), 2: list(range(-8, 9))}
B_TABBASE = {0: 0, 1: 3, 2: 8}
B_DIL = {0: 1, 1: 4, 2: 16}


def build(LS, mode="fused"):
    nc = bass.Bass("TRN2", target_bir_lowering=False)
    nL = len(LS)
    dr = {}

    def din(name, shape, dt=F32):
        dr[name] = nc.dram_tensor(name, list(shape), dt, kind="ExternalInput").ap()
        return dr[name]

    xin = din("xT", [16, 128, T])
    gains = din("gains", [128, nL * 64])
    vbin = din("vb", [128, 4])
    onesin = din("ones", [128, 128])
    if mode == "fused":
        seli = din("seli", [128, 2048], BF16)
    has = set(l % 3 for l in LS) if mode != "pre" else set()
    if 0 in has:
        tabA = din("tabA", [128, 3 * 128])
    if 1 in has:
        tabB = din("tabB", [128, 25 * 128])
    if 2 in has:
        tabC = din("tabC", [128, 5 * 512])
        cbias = din("cbias", [128, 1024])
        csig = din("csig", [128, 64])
    W = {}
    for i, l in enumerate(LS):
        m = l % 3
        nq = 6144 if m == 1 else 3072
        W[i] = {}
        if mode != "post":
            W[i]["qkv"] = din("wqkv%d" % i, [D, nq])
        if mode != "pre":
            W[i].update(wo=din("wo%d" % i, [D, D]))
            if not DEBUG_SKIP.get("ffn"):
                W[i].update(wg=din("wg%d" % i, [D, FF]), wu=din("wu%d" % i, [D, FF]), wd=din("wd%d" % i, [FF, D]))
            if m == 0:
                W[i]["sink"] = din("sink%d" % i, [128, 16])
            if m == 2:
                W[i]["lam"] = din("lam%d" % i, [128, 256])
                W[i]["subg"] = din("subg%d" % i, [128, 1])
        kvf = 2048 if m == 1 else 512
        ns = 4 if m == 2 else 3
        if mode == "pre":
            W[i]["kT_loc"] = nc.dram_tensor("kTloc", [kvf, T], BF16, kind="ExternalOutput")
            W[i]["v_loc"] = nc.dram_tensor("vloc", [T, kvf], BF16, kind="ExternalOutput")
            qout = nc.dram_tensor("qTout", [16, 128, T], BF16, kind="ExternalOutput").ap()
        elif mode == "post":
            W[i]["kT_all"] = nc.dram_tensor("kText", [ns * kvf, T], BF16, kind="ExternalInput")
            W[i]["v_all"] = nc.dram_tensor("vext", [ns * T, kvf], BF16, kind="ExternalInput")
            qin = nc.dram_tensor("qTin", [16, 128, T], BF16, kind="ExternalInput").ap()
        else:
            nb_ = kvf // 512
            W[i]["kT_loc"] = [nc.dram_tensor("kTloc%d_%d" % (i, b_), [512, T], BF16) for b_ in range(nb_)]
            W[i]["v_loc"] = [nc.dram_tensor("vloc%d_%d" % (i, b_), [T, 512], BF16) for b_ in range(nb_)]
            W[i]["kT_all"] = [nc.dram_tensor("kTall%d_%d" % (i, b_), [4 * 512, T], BF16) for b_ in range(nb_)]
            W[i]["v_all"] = [nc.dram_tensor("vall%d_%d" % (i, b_), [4 * T, 512], BF16) for b_ in range(nb_)]
    if mode != "pre":
        xout = nc.dram_tensor("outT", [16, 128, T], F32, kind="ExternalOutput").ap()

    base = (nc.sbuf_base + 63) // 64 * 64
    P = Prog(nc)

    XT = SB(nc, "XT", 0, 16, T, F32, base)
    YO = 65536
    YACC = SB(nc, "YACC", YO, 16, T, F32, base)
    HT = SB(nc, "HT", 131072, 16, T, BF16, base)
    QO = 163840
    QT = SB(nc, "QT", QO, 16, T, BF16, base)
    SO = 196608
    G = SB(nc, "G", SO, 1, nL * 64, F32, base)
    ONES = SB(nc, "ONES", SO + 1024, 1, 128, BF16, base)
    MISC = SB(nc, "MISC", SO + 1280, 1, 64, F32, base)
    RSTD = SB(nc, "RSTD", SO + 1536, 2, 512, F32, base)
    SQ = SB(nc, "SQ", SO + 5632, 2, 512, BF16, base)
    HID = SB(nc, "HID", SO + 7680, 4, T, BF16, base)
    ONESF = SB(nc, "ONESF", YO, 1, 128, F32, base)
    WGU = [(SB(nc, "WG%d" % s, QO + s * 8192, 16, 128, BF16, base), SB(nc, "WU%d" % s, QO + s * 8192 + 4096, 16, 128, BF16, base)) for s in range(2)]
    WD = [SB(nc, "WD%d" % s, QO + 16384 + s * 4096, 1, 2048, BF16, base) for s in range(4)]
    WOS = [SB(nc, "WOS%d" % s, QO + s * 16384, 16, 512, BF16, base) for s in range(2)]
    WQS = [SB(nc, "WQS%d" % s, YO + s * 16384, 16, 512, BF16, base) for s in range(3)]
    KST = [SB(nc, "KST%d" % s, YO + 49152 + s * 2048, 1, T, BF16, base) for s in range(2)]
    VST = [SB(nc, "VST%d" % s, YO + 53248 + s * 1024, 1, 512, BF16, base) for s in range(2)]
    KE = [SB(nc, "KE%d" % s, YO + s * 12288, 24, 128, BF16, base) for s in range(2)]
    VE = [SB(nc, "VE%d" % s, YO + s * 12288 + 6144, 24, 128, BF16, base) for s in range(2)]
    TAB = SB(nc, "TAB", YO + 24576, 25, 128, F32, base)
    TMP = SB(nc, "TMP", YO + 37376, 4, 128, F32, base)
    PT = SB(nc, "PT", YO + 39424, 4, 128, BF16, base)
    FIN = SB(nc, "FIN", YO + 40448, 3, 512, F32, base)
    SC = SB(nc, "SC", YO + 46592, 6, 512, F32, base)
    KC = SB(nc, "KC", YO, 1, 4096, BF16, base)
    VC = SB(nc, "VC", YO + 8192, 32, 128, BF16, base)
    TABC = SB(nc, "TABC", YO + 16384, 5, 512, F32, base)
    CB = SB(nc, "CB", YO + 26624, 1, 1024, F32, base)
    CS = SB(nc, "CS", YO + 30720, 1, 64, F32, base)
    MJ = SB(nc, "MJ", YO + 30976, 1, 512, F32, base)
    CTMP = SB(nc, "CTMP", YO + 33024, 3, 512, F32, base)
    CPT = SB(nc, "CPT", YO + 39168, 3, 512, BF16, base)
    CF = SB(nc, "CF", YO + 42240, 4, 512, F32, base)
    CSQ = SB(nc, "CSQ", YO + 50432, 1, 512, BF16, base)
    LAMT = SB(nc, "LAMT", YO + 51456, 1, 256, F32, base)
    LAMP = SB(nc, "LAMP", YO + 52480, 1, 128, F32, base)

    STG = SB(nc, "STG", SO + 7680, 4, 1024, BF16, base)
    STG4 = SB(nc, "STG4", SO + 7680, 32, 128, BF16, base)
    KEf = [SB(nc, "KEf%d" % s_, YO + s_ * 12288, 1, 3072, BF16, base) for s_ in range(2)]
    VEf = [SB(nc, "VEf%d" % s_, YO + s_ * 12288 + 6144, 1, 3072, BF16, base) for s_ in range(2)]
    VCf = SB(nc, "VCf", YO + 8192, 1, 4096, BF16, base)
    SELI_AB = SB(nc, "SELIAB", YO + 58880, 16, 128, BF16, base)
    SELI_C = SB(nc, "SELIC", YO + 61440, 16, 128, BF16, base)

    PSB = [PS(nc, "psb%d" % i, i) for i in range(8)]
    psN = PSB[0]

    for kc in range(0, 16, 4):
        P.dma("sp", XT.r(kc, a1=kc + 4), dref(xin[kc:kc + 4].rearrange("k p t -> p k t"), "xin"))
    P.dma("sp", G.r(), dref(gains, "gains"))
    P.dma("sp", MISC.r(0, 16, 20), dref(vbin, "vb"))
    P.dma("sp", ONESF.r(), dref(onesin, "ones"))
    P.copy(ONES.r(), ONESF.r())
    VB = lambda c: MISC.r(0, 16 + c, 17 + c)

    rr = {"ev": 0, "ps": 0}

    def evac(out, in_, scale=None):
        rr["ev"] ^= 1
        if rr["ev"]:
            if scale is None:
                P.act(out, in_, AF.Copy)
            else:
                P.act(out, in_, AF.Copy, scale=scale)
        else:
            if scale is None:
                P.copy(out, in_)
            else:
                P.ts(out, in_, scale, None, ALU.mult)

    def nextps(lo=1, hi=8):
        i = rr["ps"]
        rr["ps"] = (i + 1) % (hi - lo)
        return PSB[lo + i % (hi - lo)]

    def rstd_from(th, src_fn, n=16, dim=D):
        for kc in range(n):
            s = SQ.r(kc % 2)
            P.act(s, src_fn(kc), AF.Square)
            P.mm(psN.r(), ONES.r(), s, start=(kc == 0), stop=(kc == n - 1))
        P.act(RSTD.r(th), psN.r(), AF.Sqrt, bias=EPS, scale=1.0 / dim)
        P.recip(RSTD.r(th), RSTD.r(th))

    def norm_pre(gcol):
        for th in range(2):
            t0, t1 = th * 512, th * 512 + 512
            rstd_from(th, lambda kc: XT.r(kc, t0, t1))
            for kc in range(16):
                P.stt(HT.r(kc, t0, t1), XT.r(kc, t0, t1), G.r(0, gcol + kc, gcol + kc + 1), RSTD.r(th), ALU.mult, ALU.mult)

    def post_norm_res(gcol):
        for th in range(2):
            t0, t1 = th * 512, th * 512 + 512
            rstd_from(th, lambda kc: YACC.r(kc, t0, t1))
            for kc in range(16):
                P.stt(YACC.r(kc, t0, t1), YACC.r(kc, t0, t1), G.r(0, gcol + kc, gcol + kc + 1), RSTD.r(th), ALU.mult, ALU.mult)
                P.tt(XT.r(kc, t0, t1), XT.r(kc, t0, t1), YACC.r(kc, t0, t1), ALU.add)

    def load_wblock(slot, w, c0, nk=16):
        for k0 in range(0, nk, 4):
            src = w[k0 * 128:(k0 + 4) * 128, c0:c0 + 512].rearrange("(kc p) n -> p kc n", p=128)
            P.dma("pool", slot.r(k0, a1=k0 + 4), dref(src, "w"))

    def qkv_phase(i, m):
        w = W[i]["qkv"]
        nblk = 12 if m == 1 else 6
        nqb = 4
        nkb = 4 if m == 1 else 1
        qscale = 0.125 if m == 2 else 1.0 / math.sqrt(128.0)
        kT_loc, v_loc = W[i]["kT_loc"], W[i]["v_loc"]
        load_wblock(WQS[0], w, 0)
        load_wblock(WQS[1], w, 512)
        kcount = 0
        vcount = 0
        for blk in range(nblk):
            if blk + 2 < nblk:
                load_wblock(WQS[(blk + 2) % 3], w, (blk + 2) * 512)
            ws = WQS[blk % 3]
            if blk < nqb + nkb:
                for j in range(4):
                    for th in range(2):
                        ps = nextps()
                        for kc in range(16):
                            P.mm(ps.r(), ws.r(kc, j * 128, j * 128 + 128), HT.r(kc, th * 512, th * 512 + 512), start=(kc == 0), stop=(kc == 15))
                        if blk < nqb:
                            evac(QT.r(blk * 4 + j, th * 512, th * 512 + 512), ps.r(), scale=qscale)
                        else:
                            evac(KST[kcount % 2].r(0, th * 512, th * 512 + 512), ps.r())
                    if blk >= nqb:
                        kh = (blk - nqb) * 4 + j
                        if mode == "fused":
                            P.dma("sp", dref(kT_loc[kh // 4].ap()[(kh % 4) * 128:(kh % 4 + 1) * 128, :], "kTloc%d" % i), KST[kcount % 2].r())
                        else:
                            P.dma("sp", dref(kT_loc.ap()[kh * 128:(kh + 1) * 128, :], "kTloc%d" % i), KST[kcount % 2].r())
                        kcount += 1
            else:
                vb_ = blk - nqb - nkb
                for tt_ in range(8):
                    ps = nextps()
                    for kc in range(16):
                        P.mm(ps.r(), HT.r(kc, tt_ * 128, tt_ * 128 + 128), ws.r(kc, 0, 512), start=(kc == 0), stop=(kc == 15))
                    evac(VST[vcount % 2].r(), ps.r())
                    if mode == "fused":
                        P.dma("sp", dref(v_loc[vb_].ap()[tt_ * 128:(tt_ + 1) * 128, :], "vloc%d" % i), VST[vcount % 2].r())
                    else:
                        P.dma("sp", dref(v_loc.ap()[tt_ * 128:(tt_ + 1) * 128, vb_ * 512:(vb_ + 1) * 512], "vloc%d" % i), VST[vcount % 2].r())
                    vcount += 1
        for a, b in (() if mode != "fused" else (("kT_loc", "kT_all"), ("v_loc", "v_all"))):
            pass
        for a, b in () if mode != "fused" else (("kT_loc", "kT_all"), ("v_loc", "v_all")):
          for b_ in range(len(W[i][a])):
            src, dst = W[i][a][b_], W[i][b][b_]
            P.add("pool", lambda e, src=src, dst=dst: e.collective_compute(
                "AllGather", ALU.bypass, replica_groups=[[0, 1, 2, 3], [4, 5, 6, 7]],
                ins=[src.ap().opt()], outs=[dst.ap().opt()]),
                [dref(None, ("kTloc%d" if a == "kT_loc" else "vloc%d") % i)],
                [dref(None, ("kTall%d" if a == "kT_loc" else "vall%d") % i)], dma=True, inc=1, semq="cc",
                extra_deps=[(sid, n) for sid, n in P.dma_cnt.items() if sid[1] == "pool"])

    def sel(SELI, rot_s, dst, srcs, n):
        ps = nextps()
        for r in range(4):
            P.mm(ps.r(0, n), SELI.r(rot_s * 4 + r), srcs[r], start=(r == 0), stop=(r == 3))
        evac(dst, ps.r(0, n))

    def load_kv_ext_fused(i, kvf, hh, slot, tiles):
        blk_ = hh // 4
        hh = hh % 4
        kvf = 512
        kT_all, v_all = W[i]["kT_all"][blk_], W[i]["v_all"][blk_]
        kT_loc, v_loc = W[i]["kT_loc"][blk_], W[i]["v_loc"][blk_]
        for s3 in range(3):
            es = [e for e in tiles if e // 8 == s3]
            if not es:
                continue
            e0, e1 = es[0], es[-1] + 1
            ta, tb = e0 % 8, (e1 - 1) % 8 + 1
            n = tb - ta
            t0, t1 = ta * 128, tb * 128
            if s3 == 1:
                if n == 1:
                    P.dma("sp", KE[slot].r(e0), dref(kT_loc.ap()[hh * 128:(hh + 1) * 128, t0:t1], "kTloc%d" % i))
                    P.dma("sp", VE[slot].r(e0), dref(v_loc.ap()[t0:t1, hh * 128:(hh + 1) * 128], "vloc%d" % i))
                else:
                    P.dma("sp", KE[slot].r(e0, a1=e1), dref(kT_loc.ap()[hh * 128:(hh + 1) * 128, t0:t1].rearrange("d (t k) -> d t k", k=128), "kTloc%d" % i))
                    P.dma("sp", VE[slot].r(e0, a1=e1), dref(v_loc.ap()[t0:t1, hh * 128:(hh + 1) * 128].rearrange("(t p) d -> p t d", p=128), "vloc%d" % i))
                continue
            rot_s = 3 if s3 == 0 else 1
            for r in range(4):
                P.dma("sp", STG.r(r, 0, n * 128), dref(kT_all.ap()[r * kvf + hh * 128:r * kvf + hh * 128 + 128, t0:t1], "kTall%d" % i))
            for c0 in range(0, n * 128, 512):
                wd_ = min(512, n * 128 - c0)
                sel(SELI_AB, rot_s, KEf[slot].r(0, e0 * 128 + c0, e0 * 128 + c0 + wd_), [STG.r(r, c0, c0 + wd_) for r in range(4)], wd_)
            for r in range(4):
                if n == 1:
                    P.dma("sp", STG4.r(r * 8), dref(v_all.ap()[r * T + t0:r * T + t1, hh * 128:(hh + 1) * 128], "vall%d" % i))
                else:
                    P.dma("sp", STG4.r(r * 8, a1=r * 8 + n), dref(v_all.ap()[r * T + t0:r * T + t1, hh * 128:(hh + 1) * 128].rearrange("(t p) d -> p t d", p=128), "vall%d" % i))
            for c0 in range(0, n * 128, 512):
                wd_ = min(512, n * 128 - c0)
                sel(SELI_AB, rot_s, VEf[slot].r(0, e0 * 128 + c0, e0 * 128 + c0 + wd_), [STG.r(r, c0, c0 + wd_) for r in range(4)], wd_)

    def load_kv_ext(i, kvf, hh, slot, tiles):
        if mode == "fused":
            return load_kv_ext_fused(i, kvf, hh, slot, tiles)
        kT_all, v_all = W[i]["kT_all"], W[i]["v_all"]
        for s3 in range(3):
            es = [e for e in tiles if e // 8 == s3]
            if not es:
                continue
            e0, e1 = es[0], es[-1] + 1
            sh = 3 + s3
            t0, t1 = (e0 % 8) * 128, ((e1 - 1) % 8 + 1) * 128
            if e1 - e0 == 1:
                srck = lambda e, sh=sh, t0=t0, t1=t1: kT_all.ap()[(sh - 3) * kvf + hh * 128:(sh - 3) * kvf + hh * 128 + 128, t0:t1]
                P.dma("sp", KE[slot].r(e0), dref(srck, "kTall%d" % i))
                srcv = lambda e, sh=sh, t0=t0: v_all.ap()[(sh - 3) * T + t0:(sh - 3) * T + t0 + 128, hh * 128:(hh + 1) * 128]
                P.dma("sp", VE[slot].r(e0), dref(srcv, "vall%d" % i))
                continue
            srck = lambda e, sh=sh, t0=t0, t1=t1: kT_all.ap()[(sh - 3) * kvf + hh * 128:(sh - 3) * kvf + hh * 128 + 128, t0:t1].rearrange("d (t k) -> d t k", k=128)
            P.dma("sp", KE[slot].r(e0, a1=e1), dref(srck, "kTall%d" % i))
            srcv = lambda e, sh=sh, t0=t0, t1=t1: v_all.ap()[(sh - 3) * T + t0:(sh - 3) * T + t1, hh * 128:(hh + 1) * 128].rearrange("(t p) d -> p t d", p=128)
            P.dma("sp", VE[slot].r(e0, a1=e1), dref(srcv, "vall%d" % i))

    def band_attention(i, m):
        sl = slopes16()
        kvf = 2048 if m == 1 else 512
        if m == 0:
            P.dma("sp", TAB.r(0, a1=3), dref(tabA.rearrange("p (o k) -> p o k", k=128), "tabA"))
            P.dma("sp", MISC.r(0, 0, 16), dref(W[i]["sink"], "sink"))
            P.act(MISC.r(0, 0, 16), MISC.r(0, 0, 16), AF.Exp)
        else:
            P.dma("sp", TAB.r(0, a1=25), dref(tabB.rearrange("p (o k) -> p o k", k=128), "tabB"))
        if mode == "fused":
            P.dma("sp", SELI_AB.r(0, a1=16), dref(seli.rearrange("p (o k) -> p o k", k=128), "seli"))
        nkv = 4 if m == 0 else 16
        def tiles_for(g):
            offs = [-1, 0, 1] if m == 0 else B_OFFS[g]
            return sorted(set(8 + qt + o for qt in range(8) for o in offs)), offs
        g0 = 0
        tl, _ = tiles_for(0)
        load_kv_ext(i, kvf, 0, 0, tl)
        first_in_group = {}
        cnt = 0
        nkv = min(nkv, DEBUG_SKIP.get('nkv', nkv))
        for kv in range(nkv):
            slot = kv % 2
            if kv + 1 < nkv:
                gn = 0 if m == 0 else b_group(kv + 1)
                tln, _ = tiles_for(gn)
                load_kv_ext(i, kvf, kv + 1, (kv + 1) % 2, tln)
            heads = [kv * 4 + j for j in range(4)] if m == 0 else [kv]
            for h in heads:
                g = 0 if m == 0 else b_group(h)
                offs = [-1, 0, 1] if m == 0 else B_OFFS[g]
                tb = 0 if m == 0 else B_TABBASE[g]
                for qg in range(2):
                    psO = PSB[1 + (cnt % 2)]
                    psD = PSB[3 + (cnt % 2)]
                    cnt += 1
                    for qi in range(4):
                        qt = qg * 4 + qi
                        c0, c1 = qi * 128, qi * 128 + 128
                        for oi, o in enumerate(offs):
                            e = 8 + qt + o
                            k = rr.get("s", 0)
                            rr["s"] = (k + 1) % 8
                            psS = PSB[5 + k // 4 % 2] if False else PSB[5 + (k // 4)]
                            sc0 = (k % 4) * 128
                            P.mm(psS.r(sc0, sc0 + 128), KE[slot].r(e), QT.r(h, qt * 128, qt * 128 + 128))
                            tmp = TMP.r(k % 4)
                            P.stt(tmp, TAB.r(tb + oi), -sl[h], psS.r(sc0, sc0 + 128), ALU.mult, ALU.add)
                            pt = PT.r(k % 4)
                            P.act(pt, tmp, AF.Exp, bias=VB(e // 8))
                            P.mm(psO.r(c0, c1), VE[slot].r(e), pt, start=(oi == 0), stop=(oi == len(offs) - 1))
                            P.mm(psD.r(c0, c1), ONES.r(), pt, start=(oi == 0), stop=(oi == len(offs) - 1))
                    q0, q1 = qg * 512, qg * 512 + 512
                    rd = FIN.r(0)
                    if m == 0:
                        P.ts(rd, psD.r(), MISC.r(0, h, h + 1), None, ALU.add)
                        P.recip(rd, rd)
                        P.tt(HT.r(h, q0, q1), psO.r(), rd, ALU.mult)
                    else:
                        P.recip(rd, psD.r())
                        P.tt(HT.r(h, q0, q1), psO.r(), rd, ALU.mult)
                        ln = FIN.r(1)
                        if DEBUG_SKIP.get('noln'):
                            pass
                        elif (g, qg) not in first_in_group:
                            first_in_group[(g, qg)] = 1
                            P.act(SC.r(g * 2 + qg), rd, AF.Ln)
                        else:
                            P.act(ln, rd, AF.Ln)
                            P.tt(SC.r(g * 2 + qg), SC.r(g * 2 + qg), ln, ALU.add)
        if m == 1 and not DEBUG_SKIP.get('noalpha'):
            hg = [6.0, 5.0, 5.0]
            for qg in range(2):
                for g in range(3):
                    P.act(SC.r(g * 2 + qg), SC.r(g * 2 + qg), AF.Exp, scale=-1.0 / hg[g])
                tot = FIN.r(0)
                P.tt(tot, SC.r(0 + qg), SC.r(2 + qg), ALU.add)
                P.tt(tot, tot, SC.r(4 + qg), ALU.add)
                P.recip(tot, tot)
                for g in range(3):
                    P.tt(SC.r(g * 2 + qg), SC.r(g * 2 + qg), tot, ALU.mult)
                for h in range(nkv):
                    g = b_group(h)
                    P.tt(HT.r(h, qg * 512, qg * 512 + 512), HT.r(h, qg * 512, qg * 512 + 512), SC.r(g * 2 + qg), ALU.mult)

    def diff_attention(i, l):
        sl = slopes16()
        lam_init = 0.8 - 0.6 * math.exp(-0.3 * l)
        kT_all, v_all = W[i]["kT_all"], W[i]["v_all"]
        P.dma("sp", TABC.r(0, a1=5), dref(tabC.rearrange("p (o k) -> p o k", k=512), "tabC"))
        P.dma("sp", CB.r(), dref(cbias, "cbias"))
        P.dma("sp", CS.r(), dref(csig, "csig"))
        P.dma("sp", LAMT.r(), dref(W[i]["lam"], "lam"))
        P.dma("sp", LAMP.r(0, 8, 9), dref(W[i]["subg"], "subg"))
        P.tt(LAMT.r(0, 0, 64), LAMT.r(0, 0, 64), LAMT.r(0, 64, 128), ALU.mult)
        P.tt(LAMT.r(0, 128, 192), LAMT.r(0, 128, 192), LAMT.r(0, 192, 256), ALU.mult)
        P.add("dve", lambda e: e.reduce_sum(LAMP.r(0, 0, 1).ap, LAMT.r(0, 0, 64).ap, mybir.AxisListType.X), [LAMT.r(0, 0, 64)], [LAMP.r(0, 0, 1)])
        P.add("dve", lambda e: e.reduce_sum(LAMP.r(0, 1, 2).ap, LAMT.r(0, 128, 192).ap, mybir.AxisListType.X), [LAMT.r(0, 128, 192)], [LAMP.r(0, 1, 2)])
        P.act(LAMP.r(0, 0, 2), LAMP.r(0, 0, 2), AF.Exp)
        P.tt(LAMP.r(0, 2, 3), LAMP.r(0, 1, 2), LAMP.r(0, 0, 1), ALU.subtract)
        P.ts(LAMP.r(0, 2, 3), LAMP.r(0, 2, 3), -lam_init, None, ALU.add)
        P.ts(LAMP.r(0, 9, 10), LAMP.r(0, 8, 9), (1.0 - lam_init), None, ALU.mult)
        NLAM = LAMP.r(0, 2, 3)
        SG = LAMP.r(0, 9, 10)
        if mode == "fused":
            P.dma("sp", SELI_C.r(0, a1=16), dref(seli.rearrange("p (o k) -> p o k", k=128), "seli"))
        for kvh in range(4):
            if mode == "fused":
                kT_loc, v_loc = W[i]["kT_loc"][0], W[i]["v_loc"][0]
                kT_all, v_all = W[i]["kT_all"][0], W[i]["v_all"][0]
                P.dma("sp", KC.r(0, 0, 1024), dref(kT_loc.ap()[kvh * 128:(kvh + 1) * 128, :], "kTloc%d" % i))
                P.dma("sp", VC.r(0, a1=8), dref(v_loc.ap()[:, kvh * 128:(kvh + 1) * 128].rearrange("(t p) d -> p t d", p=128), "vloc%d" % i))
                for r in range(4):
                    P.dma("sp", STG.r(r), dref(kT_all.ap()[r * 512 + kvh * 128:r * 512 + kvh * 128 + 128, :], "kTall%d" % i))
                for s_ in (1, 2, 3):
                    for c0 in (0, 512):
                        sel(SELI_C, s_, KC.r(0, s_ * 1024 + c0, s_ * 1024 + c0 + 512), [STG.r(r, c0, c0 + 512) for r in range(4)], 512)
                for r in range(4):
                    P.dma("sp", STG4.r(r * 8, a1=r * 8 + 8), dref(v_all.ap()[r * T:(r + 1) * T, kvh * 128:(kvh + 1) * 128].rearrange("(t p) d -> p t d", p=128), "vall%d" % i))
                for s_ in (1, 2, 3):
                    for c0 in (0, 512):
                        sel(SELI_C, s_, VCf.r(0, s_ * 1024 + c0, s_ * 1024 + c0 + 512), [STG.r(r, c0, c0 + 512) for r in range(4)], 512)
            for s in (range(4) if mode != "fused" else ()):
                srck = lambda e, s=s, kvh=kvh: kT_all.ap()[s * 512 + kvh * 128:s * 512 + kvh * 128 + 128, :]
                P.dma("sp", KC.r(0, s * 1024, s * 1024 + 1024), dref(srck, "kTall%d" % i))
                srcv = lambda e, s=s, kvh=kvh: v_all.ap()[s * T:(s + 1) * T, kvh * 128:(kvh + 1) * 128].rearrange("(t p) d -> p t d", p=128)
                P.dma("sp", VC.r(s * 8, a1=s * 8 + 8), dref(srcv, "vall%d" % i))
            for hj in range(4):
                h = kvh * 4 + hj
                P.ts(MJ.r(), TABC.r(0), sl[h], None, ALU.mult)
                for qb in range(2):
                    q0, q1 = qb * 512, qb * 512 + 512
                    psO = [PSB[1], PSB[2]]
                    psD = [PSB[3], PSB[4]]
                    for c in range(2):
                        p0, p1 = c * 64, c * 64 + 64
                        for kt in range(32):
                            s, t = kt // 8, kt % 8
                            k = rr.get("cs", 0)
                            rr["cs"] = (k + 1) % 3
                            psS = PSB[5 + k]
                            P.mm(psS.r(), KC.r(0, kt * 128, kt * 128 + 128, p0=p0, p1=p1), QT.r(h, q0, q1, p0=p0, p1=p1))
                            tmp = CTMP.r(k)
                            pt = CPT.r(k)
                            diag = (s == 0) and (qb * 4 <= t < qb * 4 + 4)
                            col = kt * 2 + qb
                            if diag:
                                P.stt(tmp, TABC.r(1 + (t - qb * 4)), -sl[h], psS.r(), ALU.mult, ALU.add)
                                P.act(pt, tmp, AF.Exp)
                            else:
                                P.stt(tmp, MJ.r(), CS.r(0, col, col + 1), psS.r(), ALU.mult, ALU.add)
                                P.act(pt, tmp, AF.Exp, bias=CB.r(0, h * 64 + col, h * 64 + col + 1))
                            P.mm(psO[c].r(), VC.r(kt), pt, start=(kt == 0), stop=(kt == 31))
                            P.mm(psD[c].r(), ONES.r(), pt, start=(kt == 0), stop=(kt == 31))
                    r1, r2, o1, o2 = CF.r(0), CF.r(1), CF.r(2), CF.r(3)
                    P.recip(r1, psD[0].r())
                    P.tt(o1, psO[0].r(), r1, ALU.mult)
                    P.recip(r2, psD[1].r())
                    P.tt(o2, psO[1].r(), r2, ALU.mult)
                    P.stt(o1, o2, NLAM, o1, ALU.mult, ALU.add)
                    P.act(CSQ.r(), o1, AF.Square)
                    P.mm(psN.r(), ONES.r(), CSQ.r())
                    P.act(r1, psN.r(), AF.Sqrt, bias=EPS, scale=1.0 / 128.0)
                    P.recip(r1, r1)
                    P.stt(HT.r(h, q0, q1), o1, SG, r1, ALU.mult, ALU.mult)

    def wo_phase(i):
        w = W[i]["wo"]
        load_wblock(WOS[0], w, 0)
        for blk in range(4):
            if blk + 1 < 4:
                load_wblock(WOS[(blk + 1) % 2], w, (blk + 1) * 512)
            ws = WOS[blk % 2]
            for j in range(4):
                oc = blk * 4 + j
                for th in range(2):
                    ps = nextps()
                    for h in range(16):
                        P.mm(ps.r(), ws.r(h, j * 128, j * 128 + 128), HT.r(h, th * 512, th * 512 + 512), start=(h == 0), stop=(h == 15))
                    evac(YACC.r(oc, th * 512, th * 512 + 512), ps.r())

    def ffn_phase(i):
        wg, wu, wd = W[i]["wg"], W[i]["wu"], W[i]["wd"]

        def load_gu(hc):
            s = hc % 2
            for k0 in (0, 8):
                for wsrc, dst in ((wg, WGU[s][0]), (wu, WGU[s][1])):
                    src = wsrc[k0 * 128:(k0 + 8) * 128, hc * 128:(hc + 1) * 128].rearrange("(kc p) n -> p kc n", p=128)
                    P.dma("pool", dst.r(k0, a1=k0 + 8), dref(src, "w"))

        def load_d(hc):
            P.dma("pool", WD[hc % 4].r(), dref(wd[hc * 128:(hc + 1) * 128, :], "w"))

        def gateup(hc):
            s = hc % 2
            for th in range(2):
                t0, t1 = th * 512, th * 512 + 512
                pg = PSB[1 + (hc * 2 + th) % 2]
                pu = PSB[3 + (hc * 2 + th) % 2]
                for kc in range(16):
                    P.mm(pg.r(), WGU[s][0].r(kc), HT.r(kc, t0, t1), start=(kc == 0), stop=(kc == 15))
                for kc in range(16):
                    P.mm(pu.r(), WGU[s][1].r(kc), HT.r(kc, t0, t1), start=(kc == 0), stop=(kc == 15))
                sg = FINF.r((hc * 2 + th) % 2)
                P.act(sg, pg.r(), AF.Silu)
                P.tt(HID.r(hc % 4, t0, t1), sg, pu.r(), ALU.mult)

        def down(p):
            h0, h1 = 2 * p, 2 * p + 1
            for oc in range(16):
                for th in range(2):
                    t0, t1 = th * 512, th * 512 + 512
                    py = PSB[5 + (oc * 2 + th) % 3]
                    P.mm(py.r(), WD[h0 % 4].r(0, oc * 128, oc * 128 + 128), HID.r(h0 % 4, t0, t1), start=True, stop=False)
                    P.mm(py.r(), WD[h1 % 4].r(0, oc * 128, oc * 128 + 128), HID.r(h1 % 4, t0, t1), start=False, stop=True)
                    if p == 0:
                        evac(YACC.r(oc, t0, t1), py.r())
                    else:
                        P.tt(YACC.r(oc, t0, t1), YACC.r(oc, t0, t1), py.r(), ALU.add)

        npass = NHC // 2
        load_gu(0)
        load_gu(1)
        load_d(0)
        load_d(1)
        gateup(0)
        gateup(1)
        for p in range(npass):
            if p + 1 < npass:
                for hc in (2 * p + 2, 2 * p + 3):
                    load_gu(hc)
                    load_d(hc)
                    gateup(hc)
            down(p)

    FINF = SB(nc, "FINF", SO + 1536, 2, 512, F32, base)

    toks = []
    for i, l in enumerate(LS):
        m = l % 3
        if mode != "post":
            norm_pre(i * 64 + 0)
            qkv_phase(i, m)
        if mode == "pre":
            for kc in range(0, 16, 4):
                toks.append(P.dma("sp", dref(qout[kc:kc + 4].rearrange("k p t -> p k t"), "qout"), QT.r(kc, a1=kc + 4)))
            continue
        if mode == "post":
            for kc in range(0, 16, 4):
                P.dma("sp", QT.r(kc, a1=kc + 4), dref(qin[kc:kc + 4].rearrange("k p t -> p k t"), "qin"))
        if DEBUG_SKIP.get("att"):
            pass
        elif m == 2:
            diff_attention(i, l)
        else:
            band_attention(i, m)
        wo_phase(i)
        post_norm_res(i * 64 + 16)
        if not DEBUG_SKIP.get("ffn"):
            norm_pre(i * 64 + 32)
            ffn_phase(i)
            post_norm_res(i * 64 + 48)

    if mode == "pre":
        for sid, n in list(P.dma_cnt.items()):
            if sid[1] == "sp":
                toks.append((sid, n))
    else:
        for kc in range(0, 16, 4):
            toks.append(P.dma("sp", dref(xout[kc:kc + 4].rearrange("k p t -> p k t"), "xout"), XT.r(kc, a1=kc + 4)))
    P.emit(toks)
    return nc


LAYER_GROUPS = [[0], [1], [2], [3]]
DEBUG_SKIP = {}
N_LAYERS = 4
_cache = {}


def _tables():
    ii = np.arange(128)[:, None].astype(np.float64)
    jj = np.arange(128)[None, :].astype(np.float64)
    tabA = np.zeros((128, 3, 128), np.float32)
    for oi, o in enumerate((-1, 0, 1)):
        d = o * 128 + ii - jj
        tabA[:, oi, :] = np.where(np.abs(d) <= 128, np.abs(d), BIG)
    tabB = np.zeros((128, 25, 128), np.float32)
    for g in range(3):
        dil = B_DIL[g]
        for oi, o in enumerate(B_OFFS[g]):
            d = o * 128 + ii - jj
            ok = (np.abs(d) <= 64 * dil) & (np.mod(d, dil) == 0)
            tabB[:, B_TABBASE[g] + oi, :] = np.where(ok, np.abs(d), BIG)
    j5 = np.arange(512)[None, :].astype(np.float64)
    tabC = np.zeros((128, 5, 512), np.float32)
    tabC[:, 0, :] = np.broadcast_to(j5, (128, 512))
    for c in range(4):
        tabC[:, 1 + c, :] = np.abs(ii - j5 + 128 * c)
    return tabA.reshape(128, -1), tabB.reshape(128, -1), tabC.reshape(128, -1)


def _c_tables(q):
    sl = slopes16()
    ii = np.arange(128).astype(np.float64)
    csig = np.zeros((128, 64), np.float32)
    cbias = np.zeros((128, 16, 64), np.float32)
    for s in range(4):
        for t in range(8):
            for qb in range(2):
                col = (s * 8 + t) * 2 + qb
                kabs0 = ((q + s) % 4) * 1024 + t * 128
                qabs0 = q * 1024 + qb * 512
                Dk = kabs0 - qabs0
                if Dk >= 512:
                    csig[:, col] = 1.0
                    for h in range(16):
                        cbias[:, h, col] = -sl[h] * (Dk + ii)
                elif Dk <= -128:
                    csig[:, col] = -1.0
                    for h in range(16):
                        cbias[:, h, col] = sl[h] * (Dk + ii)
    return csig, cbias.reshape(128, 1024)


def _get_nc(l, mode):
    key = (l % 3, mode)
    if key not in _cache:
        _cache[key] = build([l], mode)
    return _cache[key]


FUSED = True


def _kernel_fused(x, mix_pre_g, mix_post_g, ffn_pre_g, ffn_post_g, w_gate, w_up, w_down,
                  a_w_qkv, a_w_o, a_sink, b_w_qkv, b_w_o, c_w_qkv, c_w_o,
                  c_lambda_q1, c_lambda_k1, c_lambda_q2, c_lambda_k2, c_subln_g):
    f = lambda a: np.ascontiguousarray(np.asarray(a), dtype=np.float32)
    x = f(x)
    gl = [f(mix_pre_g), f(mix_post_g), f(ffn_pre_g), f(ffn_post_g)]
    tabA, tabB, tabC = _tables()
    LS = list(range(N_LAYERS))
    key = ("fused", tuple(LS))
    if key not in _cache:
        _cache[key] = build(LS, "fused")
    nc = _cache[key]
    shared = {"ones": np.ones((128, 128), np.float32)}
    has = set(l % 3 for l in LS)
    if 0 in has:
        shared["tabA"] = tabA
    if 1 in has:
        shared["tabB"] = tabB
    if 2 in has:
        shared["tabC"] = tabC
    gains = np.zeros((128, len(LS) * 64), np.float32)
    for i, l in enumerate(LS):
        m, j = l % 3, l // 3
        for n in range(4):
            gains[:, i * 64 + n * 16:i * 64 + n * 16 + 16] = gl[n][l].reshape(16, 128).T
        if not DEBUG_SKIP.get("ffn"):
            shared["wg%d" % i] = f(w_gate[l])
            shared["wu%d" % i] = f(w_up[l])
            shared["wd%d" % i] = f(w_down[l])
        shared["wqkv%d" % i] = f((a_w_qkv, b_w_qkv, c_w_qkv)[m][j])
        shared["wo%d" % i] = f((a_w_o, b_w_o, c_w_o)[m][j])
        if m == 0:
            shared["sink%d" % i] = np.ascontiguousarray(np.broadcast_to(f(a_sink[j])[None, :], (128, 16)))
        elif m == 2:
            lam = np.concatenate([f(c_lambda_q1[j]), f(c_lambda_k1[j]), f(c_lambda_q2[j]), f(c_lambda_k2[j])])
            shared["lam%d" % i] = np.ascontiguousarray(np.broadcast_to(lam[None, :], (128, 256)))
            shared["subg%d" % i] = np.ascontiguousarray(f(c_subln_g[j]).reshape(128, 1))
    shared["gains"] = gains
    in_maps = []
    eye = np.eye(128, dtype=np.float32)
    for c in range(8):
        b, q = c // 4, c % 4
        d = dict(shared)
        d["xT"] = np.ascontiguousarray(x[b, q * 1024:(q + 1) * 1024, :].T).reshape(16, 128, 1024)
        vb = np.zeros((128, 4), np.float32)
        if q == 0:
            vb[:, 0] = NEGV
        if q == 3:
            vb[:, 2] = NEGV
        d["vb"] = vb
        se = np.zeros((128, 16, 128), np.float32)
        for s_ in range(4):
            se[:, s_ * 4 + (q + s_) % 4, :] = eye
        d["seli"] = se.reshape(128, 2048).astype(ml_dtypes.bfloat16)
        if 2 in has:
            cs, cb = _c_tables(q)
            d["csig"] = cs
            d["cbias"] = cb
        in_maps.append(d)
    res = run_bass_kernel_spmd(nc, in_maps, core_ids=list(range(8)))
    out = np.zeros((2, 4096, 2048), np.float32)
    for c in range(8):
        b, q = c // 4, c % 4
        out[b, q * 1024:(q + 1) * 1024, :] = np.asarray(res.results[c]["outT"]).reshape(2048, 1024).T
    return out


def kernel(x, mix_pre_g, mix_post_g, ffn_pre_g, ffn_post_g, w_gate, w_up, w_down,
           a_w_qkv, a_w_o, a_sink, b_w_qkv, b_w_o, c_w_qkv, c_w_o,
           c_lambda_q1, c_lambda_k1, c_lambda_q2, c_lambda_k2, c_subln_g):
    if FUSED:
        return _kernel_fused(x, mix_pre_g, mix_post_g, ffn_pre_g, ffn_post_g, w_gate, w_up, w_down,
                             a_w_qkv, a_w_o, a_sink, b_w_qkv, b_w_o, c_w_qkv, c_w_o,
                             c_lambda_q1, c_lambda_k1, c_lambda_q2, c_lambda_k2, c_subln_g)
    f = lambda a: np.ascontiguousarray(np.asarray(a), dtype=np.float32)
    x = f(x)
    gl = [f(mix_pre_g), f(mix_post_g), f(ffn_pre_g), f(ffn_post_g)]
    tabA, tabB, tabC = _tables()
    ones = np.ones((128, 128), np.float32)
    xT = []
    for c in range(8):
        b, q = c // 4, c % 4
        xT.append(np.ascontiguousarray(x[b, q * 1024:(q + 1) * 1024, :].T).reshape(16, 128, 1024))
    for l in range(N_LAYERS):
        m, j = l % 3, l // 3
        gains = np.zeros((128, 64), np.float32)
        for n in range(4):
            gains[:, n * 16:n * 16 + 16] = gl[n][l].reshape(16, 128).T
        wq = f((a_w_qkv, b_w_qkv, c_w_qkv)[m][j])
        wo = f((a_w_o, b_w_o, c_w_o)[m][j])
        common = {"ones": ones, "gains": gains}
        nc = _get_nc(l, "pre")
        in_maps = []
        for c in range(8):
            d = dict(common)
            d["xT"] = xT[c]
            d["vb"] = np.zeros((128, 4), np.float32)
            d["wqkv0"] = wq
            in_maps.append(d)
        res = run_bass_kernel_spmd(nc, in_maps, core_ids=list(range(8)))
        qT = [np.asarray(res.results[c]["qTout"]) for c in range(8)]
        kT = [np.asarray(res.results[c]["kTloc"]) for c in range(8)]
        vv = [np.asarray(res.results[c]["vloc"]) for c in range(8)]
        nc = _get_nc(l, "post")
        in_maps = []
        for c in range(8):
            b, q = c // 4, c % 4
            d = dict(common)
            d["xT"] = xT[c]
            d["qTin"] = qT[c]
            if m == 2:
                qs = [(q + s_) % 4 for s_ in range(4)]
            else:
                qs = [(q + 3) % 4, q, (q + 1) % 4]
            d["kText"] = np.ascontiguousarray(np.concatenate([kT[b * 4 + r] for r in qs], axis=0))
            d["vext"] = np.ascontiguousarray(np.concatenate([vv[b * 4 + r] for r in qs], axis=0))
            vb = np.zeros((128, 4), np.float32)
            if q == 0:
                vb[:, 0] = NEGV
            if q == 3:
                vb[:, 2] = NEGV
            d["vb"] = vb
            d["wo0"] = wo
            d["wg0"] = f(w_gate[l])
            d["wu0"] = f(w_up[l])
            d["wd0"] = f(w_down[l])
            if m == 0:
                d["tabA"] = tabA
                d["sink0"] = np.ascontiguousarray(np.broadcast_to(f(a_sink[j])[None, :], (128, 16)))
            elif m == 1:
                d["tabB"] = tabB
            else:
                d["tabC"] = tabC
                lam = np.concatenate([f(c_lambda_q1[j]), f(c_lambda_k1[j]), f(c_lambda_q2[j]), f(c_lambda_k2[j])])
                d["lam0"] = np.ascontiguousarray(np.broadcast_to(lam[None, :], (128, 256)))
                d["subg0"] = np.ascontiguousarray(f(c_subln_g[j]).reshape(128, 1))
                cs, cb = _c_tables(q)
                d["csig"] = cs
                d["cbias"] = cb
            in_maps.append(d)
        res = run_bass_kernel_spmd(nc, in_maps, core_ids=list(range(8)))
        xT = [np.asarray(res.results[c]["outT"]) for c in range(8)]
    out = np.zeros((2, 4096, 2048), np.float32)
    for c in range(8):
        b, q = c // 4, c % 4
        out[b, q * 1024:(q + 1) * 1024, :] = xT[c].reshape(2048, 1024).T
    return out
```

```python
import numpy as np
import concourse.bass as bass
import concourse.mybir as mybir
from concourse.bass_utils import run_bass_kernel_spmd

F32 = mybir.dt.float32
BF16 = mybir.dt.bfloat16
I32 = mybir.dt.int32
AF = mybir.ActivationFunctionType
ALU = mybir.AluOpType
ESZ = {F32: 4, BF16: 2, I32: 4}
CELL = 512


class Ref:
    __slots__ = ("ap", "keys")

    def __init__(self, ap, keys):
        self.ap = ap
        self.keys = keys


class SB:
    def __init__(self, nc, name, off, A, B, dt, base):
        self.A, self.B, self.dt, self.off = A, B, dt, off
        self.esz = ESZ[dt]
        self.h = nc.alloc_sbuf_tensor_at(name, [128, A, B], dt, offset=base + off)
        self.nbytes = A * B * self.esz

    def r(self, a=0, b0=0, b1=None, p0=0, p1=128, a1=None):
        if b1 is None:
            b1 = self.B
        if a1 is None:
            ap = self.h[p0:p1, a, b0:b1]
            lo = self.off + (a * self.B + b0) * self.esz
            hi = self.off + (a * self.B + b1) * self.esz
        else:
            ap = self.h[p0:p1, a:a1, b0:b1]
            lo = self.off + (a * self.B + b0) * self.esz
            hi = self.off + ((a1 - 1) * self.B + b1) * self.esz
        return Ref(ap, [("sb", c) for c in range(lo // CELL, (hi - 1) // CELL + 1)])


class PS:
    def __init__(self, nc, name, idx):
        self.h = nc.alloc_psum_tensor(name, [128, 512], F32)
        self.idx = idx

    def r(self, b0=0, b1=512, p0=0, p1=128):
        lo, hi = b0 * 4, b1 * 4
        return Ref(self.h[p0:p1, b0:b1], [("ps", self.idx, c) for c in range(lo // CELL, (hi - 1) // CELL + 1)])


def dref(ap, name):
    return Ref(ap, [("dram", name)])


COMPUTE = ("pe", "act", "dve", "pool")


class Prog:
    def __init__(self, nc, n_dma_sems=12):
        self.nc = nc
        self.streams = {k: [] for k in ("pe", "act", "dve", "pool", "sp")}
        self.last_w = {}
        self.readers = {}
        self.n_dma_sems = n_dma_sems
        self.dma_rr = {"sp": 0, "pool": 0, "act": 0, "cc": 0}
        self.dma_cnt = {}
        self.dma_inc = {}
        self.flag = set()
        self.order = []

    def add(self, eng, fn, reads=(), writes=(), dma=None, inc=16, extra_deps=(), semq=None):
        deps = {}

        def need(tok):
            if tok is None:
                return
            s, i = tok
            if s == eng and eng == "pe" and dma is None:
                return
            if deps.get(s, -1) < i:
                deps[s] = i

        for r in reads:
            for k in r.keys:
                need(self.last_w.get(k))
        for w in writes:
            for k in w.keys:
                need(self.last_w.get(k))
                rd = self.readers.get(k)
                if rd:
                    for s, i in rd.items():
                        need((s, i))
        for t in extra_deps:
            need(t)
        if dma is not None:
            q = semq or eng
            j = self.dma_rr[q]
            self.dma_rr[q] = (j + 1) % (2 if q == "cc" else self.n_dma_sems)
            sid = ("dma", q, j)
            n = self.dma_cnt.get(sid, 0) + 1
            self.dma_cnt[sid] = n
            self.dma_inc[sid] = inc
            if n > 1:
                need((sid, n - 1))
            tok = (sid, n)
            self.streams[eng].append(dict(fn=fn, deps=deps, dma=sid, inc=inc))
        else:
            idx = len(self.streams[eng])
            tok = (eng, idx)
            self.streams[eng].append(dict(fn=fn, deps=deps, dma=None))
        for s, i in deps.items():
            if s in COMPUTE:
                self.flag.add((s, i))
        for r in reads:
            for k in r.keys:
                self.readers.setdefault(k, {})[tok[0]] = tok[1]
        for w in writes:
            for k in w.keys:
                self.last_w[k] = tok
                self.readers[k] = {}
        return tok

    def emit(self, final_wait_tokens):
        nc = self.nc
        sems = {}
        import contextlib
        with contextlib.ExitStack() as st:
            for e in COMPUTE:
                sems[e] = st.enter_context(nc.semaphore("s_" + e))
            for sid in self.dma_cnt:
                sems[sid] = st.enter_context(nc.semaphore("d_%s_%d" % (sid[1], sid[2])))
            block = st.enter_context(nc.Block())
            cum = {}
            for e in COMPUTE:
                c = 0
                arr = []
                for i in range(len(self.streams[e])):
                    if (e, i) in self.flag:
                        c += 1
                    arr.append(c)
                cum[e] = arr

            def val(s, i):
                if s in COMPUTE:
                    return cum[s][i]
                return self.dma_inc[s] * i

            def run(ename, eng):
                waited = {}
                for i, ins in enumerate(self.streams[ename]):
                    for s, di in ins["deps"].items():
                        v = val(s, di)
                        if waited.get(s, 0) < v:
                            eng.wait_ge(sems[s], v)
                            waited[s] = v
                    r = ins["fn"](eng)
                    if ins["dma"] is not None:
                        r.then_inc(sems[ins["dma"]], ins["inc"])
                    elif (ename, i) in self.flag:
                        r.then_inc(sems[ename], 1)
                if ename == "sp":
                    for s, di in final_wait_tokens:
                        v = val(s, di)
                        if waited.get(s, 0) < v:
                            eng.wait_ge(sems[s], v)
                            waited[s] = v

            @block.tensor
            def _(e):
                run("pe", e)

            @block.scalar
            def _(e):
                run("act", e)

            @block.vector
            def _(e):
                run("dve", e)

            @block.gpsimd
            def _(e):
                run("pool", e)

            @block.sync
            def _(e):
                run("sp", e)

    def getpid(self, e):
        if not hasattr(self, "_pids"):
            self._pids = {}
        k = id(e)
        if k not in self._pids:
            self._pids[k] = e.partition_id()
        return self._pids[k]

    def mm(self, out, lhsT, rhs, start=True, stop=True):
        rd = [lhsT, rhs] + ([] if start else [out])
        return self.add("pe", lambda e: e.matmul(out.ap, lhsT.ap, rhs.ap, start=start, stop=stop), rd, [out])

    def act(self, out, in_, func, bias=None, scale=1.0):
        rd = [in_] + ([bias] if isinstance(bias, Ref) else [])
        if bias is None:
            return self.add("act", lambda e: e.activation(out.ap, in_.ap, func, scale=scale), rd, [out])
        b = bias.ap if isinstance(bias, Ref) else bias
        return self.add("act", lambda e: e.activation(out.ap, in_.ap, func, bias=b, scale=scale), rd, [out])

    def tt(self, out, a, b, op, eng="dve"):
        return self.add(eng, lambda e: e.tensor_tensor(out.ap, a.ap, b.ap, op), [a, b], [out])

    def stt(self, out, in0, scalar, in1, op0, op1, eng="dve"):
        rd = [in0, in1] + ([scalar] if isinstance(scalar, Ref) else [])
        s = scalar.ap if isinstance(scalar, Ref) else scalar
        return self.add(eng, lambda e: e.scalar_tensor_tensor(out.ap, in0.ap, s, in1.ap, op0, op1), rd, [out])

    def ts(self, out, in0, s1, s2, op0, op1=None, eng="dve"):
        rd = [in0] + [x for x in (s1, s2) if isinstance(x, Ref)]
        a1 = s1.ap if isinstance(s1, Ref) else s1
        a2 = s2.ap if isinstance(s2, Ref) else s2
        if op1 is None:
            return self.add(eng, lambda e: e.tensor_scalar(out.ap, in0.ap, a1, None, op0), rd, [out])
        return self.add(eng, lambda e: e.tensor_scalar(out.ap, in0.ap, a1, a2, op0, op1), rd, [out])

    def copy(self, out, in_, eng="dve"):
        return self.add(eng, lambda e: e.tensor_copy(out.ap, in_.ap), [in_], [out])

    def recip(self, out, in_):
        return self.add("dve", lambda e: e.reciprocal(out.ap, in_.ap), [in_], [out])

    def dma(self, q, out, in_):
        def fn(e):
            o = out.ap(e) if callable(out.ap) else out.ap
            i = in_.ap(e) if callable(in_.ap) else in_.ap
            return e.dma_start(out=o, in_=i)
        return self.add(q, fn, [in_], [out], dma=True)


import math
import ml_dtypes

D = 2048
T = 1024
NH = 16
FF = 5632
NHC = FF // 128
EPS = 1e-6
BIG = 1.0e9
NEGV = -30000.0


def slopes16():
    return [2.0 ** (-8.0 * (h + 1) / 16.0) for h in range(16)]


def b_group(h):
    return 0 if h < 6 else (1 if h < 11 else 2)


B_OFFS = {0: list(range(-1, 2)), 1: list(range(-2, 3)), 2: list(range(-8, 9))}
B_TABBASE = {0: 0, 1: 3, 2: 8}
B_DIL = {0: 1, 1: 4, 2: 16}


def build(LS, mode="fused"):
    nc = bass.Bass("TRN2", target_bir_lowering=False)
    nL = len(LS)
    dr = {}

    def din(name, shape, dt=F32):
        dr[name] = nc.dram_tensor(name, list(shape), dt, kind="ExternalInput").ap()
        return dr[name]

    xin = din("xT", [16, 128, T])
    gains = din("gains", [128, nL * 64])
    vbin = din("vb", [128, 4])
    onesin = din("ones", [128, 128])
    if mode == "fused":
        seli = din("seli", [128, 2048], BF16)
    has = set(l % 3 for l in LS) if mode != "pre" else set()
    if 0 in has:
        tabA = din("tabA", [128, 3 * 128])
    if 1 in has:
        tabB = din("tabB", [128, 25 * 128])
    if 2 in has:
        tabC = din("tabC", [128, 5 * 512])
        cbias = din("cbias", [128, 1024])
        csig = din("csig", [128, 64])
    W = {}
    for i, l in enumerate(LS):
        m = l % 3
        nq = 6144 if m == 1 else 3072
        W[i] = {}
        if mode != "post":
            W[i]["qkv"] = din("wqkv%d" % i, [D, nq])
        if mode != "pre":
            W[i].update(wo=din("wo%d" % i, [D, D]))
            if not DEBUG_SKIP.get("ffn"):
                W[i].update(wg=din("wg%d" % i, [D, FF]), wu=din("wu%d" % i, [D, FF]), wd=din("wd%d" % i, [FF, D]))
            if m == 0:
                W[i]["sink"] = din("sink%d" % i, [128, 16])
            if m == 2:
                W[i]["lam"] = din("lam%d" % i, [128, 256])
                W[i]["subg"] = din("subg%d" % i, [128, 1])
        kvf = 2048 if m == 1 else 512
        ns = 4 if m == 2 else 3
        if mode == "pre":
            W[i]["kT_loc"] = nc.dram_tensor("kTloc", [kvf, T], BF16, kind="ExternalOutput")
            W[i]["v_loc"] = nc.dram_tensor("vloc", [T, kvf], BF16, kind="ExternalOutput")
            qout = nc.dram_tensor("qTout", [16, 128, T], BF16, kind="ExternalOutput").ap()
        elif mode == "post":
            W[i]["kT_all"] = nc.dram_tensor("kText", [ns * kvf, T], BF16, kind="ExternalInput")
            W[i]["v_all"] = nc.dram_tensor("vext", [ns * T, kvf], BF16, kind="ExternalInput")
            qin = nc.dram_tensor("qTin", [16, 128, T], BF16, kind="ExternalInput").ap()
        else:
            nb_ = kvf // 512
            W[i]["kT_loc"] = [nc.dram_tensor("kTloc%d_%d" % (i, b_), [512, T], BF16) for b_ in range(nb_)]
            W[i]["v_loc"] = [nc.dram_tensor("vloc%d_%d" % (i, b_), [T, 512], BF16) for b_ in range(nb_)]
            W[i]["kT_all"] = [nc.dram_tensor("kTall%d_%d" % (i, b_), [4 * 512, T], BF16) for b_ in range(nb_)]
            W[i]["v_all"] = [nc.dram_tensor("vall%d_%d" % (i, b_), [4 * T, 512], BF16) for b_ in range(nb_)]
    if mode != "pre":
        xout = nc.dram_tensor("outT", [16, 128, T], F32, kind="ExternalOutput").ap()

    base = (nc.sbuf_base + 63) // 64 * 64
    P = Prog(nc)

    XT = SB(nc, "XT", 0, 16, T, F32, base)
    YO = 65536
    YACC = SB(nc, "YACC", YO, 16, T, F32, base)
    HT = SB(nc, "HT", 131072, 16, T, BF16, base)
    QO = 163840
    QT = SB(nc, "QT", QO, 16, T, BF16, base)
    SO = 196608
    G = SB(nc, "G", SO, 1, nL * 64, F32, base)
    ONES = SB(nc, "ONES", SO + 1024, 1, 128, BF16, base)
    MISC = SB(nc, "MISC", SO + 1280, 1, 64, F32, base)
    RSTD = SB(nc, "RSTD", SO + 1536, 2, 512, F32, base)
    SQ = SB(nc, "SQ", SO + 5632, 2, 512, BF16, base)
    HID = SB(nc, "HID", SO + 7680, 4, T, BF16, base)
    ONESF = SB(nc, "ONESF", YO, 1, 128, F32, base)
    WGU = [(SB(nc, "WG%d" % s, QO + s * 8192, 16, 128, BF16, base), SB(nc, "WU%d" % s, QO + s * 8192 + 4096, 16, 128, BF16, base)) for s in range(2)]
    WD = [SB(nc, "WD%d" % s, QO + 16384 + s * 4096, 1, 2048, BF16, base) for s in range(4)]
    WOS = [SB(nc, "WOS%d" % s, QO + s * 16384, 16, 512, BF16, base) for s in range(2)]
    WQS = [SB(nc, "WQS%d" % s, YO + s * 16384, 16, 512, BF16, base) for s in range(3)]
    KST = [SB(nc, "KST%d" % s, YO + 49152 + s * 2048, 1, T, BF16, base) for s in range(2)]
    VST = [SB(nc, "VST%d" % s, YO + 53248 + s * 1024, 1, 512, BF16, base) for s in range(2)]
    KE = [SB(nc, "KE%d" % s, YO + s * 12288, 24, 128, BF16, base) for s in range(2)]
    VE = [SB(nc, "VE%d" % s, YO + s * 12288 + 6144, 24, 128, BF16, base) for s in range(2)]
    TAB = SB(nc, "TAB", YO + 24576, 25, 128, F32, base)
    TMP = SB(nc, "TMP", YO + 37376, 4, 128, F32, base)
    PT = SB(nc, "PT", YO + 39424, 4, 128, BF16, base)
    FIN = SB(nc, "FIN", YO + 40448, 3, 512, F32, base)
    SC = SB(nc, "SC", YO + 46592, 6, 512, F32, base)
    KC = SB(nc, "KC", YO, 1, 4096, BF16, base)
    VC = SB(nc, "VC", YO + 8192, 32, 128, BF16, base)
    TABC = SB(nc, "TABC", YO + 16384, 5, 512, F32, base)
    CB = SB(nc, "CB", YO + 26624, 1, 1024, F32, base)
    CS = SB(nc, "CS", YO + 30720, 1, 64, F32, base)
    MJ = SB(nc, "MJ", YO + 30976, 1, 512, F32, base)
    CTMP = SB(nc, "CTMP", YO + 33024, 3, 512, F32, base)
    CPT = SB(nc, "CPT", YO + 39168, 3, 512, BF16, base)
    CF = SB(nc, "CF", YO + 42240, 4, 512, F32, base)
    CSQ = SB(nc, "CSQ", YO + 50432, 1, 512, BF16, base)
    LAMT = SB(nc, "LAMT", YO + 51456, 1, 256, F32, base)
    LAMP = SB(nc, "LAMP", YO + 52480, 1, 128, F32, base)

    STG = SB(nc, "STG", SO + 7680, 4, 1024, BF16, base)
    STG4 = SB(nc, "STG4", SO + 7680, 32, 128, BF16, base)
    KEf = [SB(nc, "KEf%d" % s_, YO + s_ * 12288, 1, 3072, BF16, base) for s_ in range(2)]
    VEf = [SB(nc, "VEf%d" % s_, YO + s_ * 12288 + 6144, 1, 3072, BF16, base) for s_ in range(2)]
    VCf = SB(nc, "VCf", YO + 8192, 1, 4096, BF16, base)
    SELI_AB = SB(nc, "SELIAB", YO + 58880, 16, 128, BF16, base)
    SELI_C = SB(nc, "SELIC", YO + 61440, 16, 128, BF16, base)

    PSB = [PS(nc, "psb%d" % i, i) for i in range(8)]
    psN = PSB[0]

    for kc in range(0, 16, 4):
        P.dma("sp", XT.r(kc, a1=kc + 4), dref(xin[kc:kc + 4].rearrange("k p t -> p k t"), "xin"))
    P.dma("sp", G.r(), dref(gains, "gains"))
    P.dma("sp", MISC.r(0, 16, 20), dref(vbin, "vb"))
    P.dma("sp", ONESF.r(), dref(onesin, "ones"))
    P.copy(ONES.r(), ONESF.r())
    VB = lambda c: MISC.r(0, 16 + c, 17 + c)

    rr = {"ev": 0, "ps": 0}

    def evac(out, in_, scale=None):
        rr["ev"] ^= 1
        if rr["ev"]:
            if scale is None:
                P.act(out, in_, AF.Copy)
            else:
                P.act(out, in_, AF.Copy, scale=scale)
        else:
            if scale is None:
                P.copy(out, in_)
            else:
                P.ts(out, in_, scale, None, ALU.mult)

    def nextps(lo=1, hi=8):
        i = rr["ps"]
        rr["ps"] = (i + 1) % (hi - lo)
        return PSB[lo + i % (hi - lo)]

    def rstd_from(th, src_fn, n=16, dim=D):
        for kc in range(n):
            s = SQ.r(kc % 2)
            P.act(s, src_fn(kc), AF.Square)
            P.mm(psN.r(), ONES.r(), s, start=(kc == 0), stop=(kc == n - 1))
        P.act(RSTD.r(th), psN.r(), AF.Sqrt, bias=EPS, scale=1.0 / dim)
        P.recip(RSTD.r(th), RSTD.r(th))

    def norm_pre(gcol):
        for th in range(2):
            t0, t1 = th * 512, th * 512 + 512
            rstd_from(th, lambda kc: XT.r(kc, t0, t1))
            for kc in range(16):
                P.stt(HT.r(kc, t0, t1), XT.r(kc, t0, t1), G.r(0, gcol + kc, gcol + kc + 1), RSTD.r(th), ALU.mult, ALU.mult)

    def post_norm_res(gcol):
        for th in range(2):
            t0, t1 = th * 512, th * 512 + 512
            rstd_from(th, lambda kc: YACC.r(kc, t0, t1))
            for kc in range(16):
                P.stt(YACC.r(kc, t0, t1), YACC.r(kc, t0, t1), G.r(0, gcol + kc, gcol + kc + 1), RSTD.r(th), ALU.mult, ALU.mult)
                P.tt(XT.r(kc, t0, t1), XT.r(kc, t0, t1), YACC.r(kc, t0, t1), ALU.add)

    def load_wblock(slot, w, c0, nk=16):
        for k0 in range(0, nk, 4):
            src = w[k0 * 128:(k0 + 4) * 128, c0:c0 + 512].rearrange("(kc p) n -> p kc n", p=128)
            P.dma("pool", slot.r(k0, a1=k0 + 4), dref(src, "w"))

    def qkv_phase(i, m):
        w = W[i]["qkv"]
        nblk = 12 if m == 1 else 6
        nqb = 4
        nkb = 4 if m == 1 else 1
        qscale = 0.125 if m == 2 else 1.0 / math.sqrt(128.0)
        kT_loc, v_loc = W[i]["kT_loc"], W[i]["v_loc"]
        load_wblock(WQS[0], w, 0)
        load_wblock(WQS[1], w, 512)
        kcount = 0
        vcount = 0
        for blk in range(nblk):
            if blk + 2 < nblk:
                load_wblock(WQS[(blk + 2) % 3], w, (blk + 2) * 512)
            ws = WQS[blk % 3]
            if blk < nqb + nkb:
                for j in range(4):
                    for th in range(2):
                        ps = nextps()
                        for kc in range(16):
                            P.mm(ps.r(), ws.r(kc, j * 128, j * 128 + 128), HT.r(kc, th * 512, th * 512 + 512), start=(kc == 0), stop=(kc == 15))
                        if blk < nqb:
                            evac(QT.r(blk * 4 + j, th * 512, th * 512 + 512), ps.r(), scale=qscale)
                        else:
                            evac(KST[kcount % 2].r(0, th * 512, th * 512 + 512), ps.r())
                    if blk >= nqb:
                        kh = (blk - nqb) * 4 + j
                        if mode == "fused":
                            P.dma("sp", dref(kT_loc[kh // 4].ap()[(kh % 4) * 128:(kh % 4 + 1) * 128, :], "kTloc%d" % i), KST[kcount % 2].r())
                        else:
                            P.dma("sp", dref(kT_loc.ap()[kh * 128:(kh + 1) * 128, :], "kTloc%d" % i), KST[kcount % 2].r())
                        kcount += 1
            else:
                vb_ = blk - nqb - nkb
                for tt_ in range(8):
                    ps = nextps()
                    for kc in range(16):
                        P.mm(ps.r(), HT.r(kc, tt_ * 128, tt_ * 128 + 128), ws.r(kc, 0, 512), start=(kc == 0), stop=(kc == 15))
                    evac(VST[vcount % 2].r(), ps.r())
                    if mode == "fused":
                        P.dma("sp", dref(v_loc[vb_].ap()[tt_ * 128:(tt_ + 1) * 128, :], "vloc%d" % i), VST[vcount % 2].r())
                    else:
                        P.dma("sp", dref(v_loc.ap()[tt_ * 128:(tt_ + 1) * 128, vb_ * 512:(vb_ + 1) * 512], "vloc%d" % i), VST[vcount % 2].r())
                    vcount += 1
        for a, b in (() if mode != "fused" else (("kT_loc", "kT_all"), ("v_loc", "v_all"))):
            pass
        for a, b in () if mode != "fused" else (("kT_loc", "kT_all"), ("v_loc", "v_all")):
          for b_ in range(len(W[i][a])):
            src, dst = W[i][a][b_], W[i][b][b_]
            P.add("pool", lambda e, src=src, dst=dst: e.collective_compute(
                "AllGather", ALU.bypass, replica_groups=[[0, 1, 2, 3], [4, 5, 6, 7]],
                ins=[src.ap().opt()], outs=[dst.ap().opt()]),
                [dref(None, ("kTloc%d" if a == "kT_loc" else "vloc%d") % i)],
                [dref(None, ("kTall%d" if a == "kT_loc" else "vall%d") % i)], dma=True, inc=1, semq="cc",
                extra_deps=[(sid, n) for sid, n in P.dma_cnt.items() if sid[1] == "pool"])

    def sel(SELI, rot_s, dst, srcs, n):
        ps = nextps()
        for r in range(4):
            P.mm(ps.r(0, n), SELI.r(rot_s * 4 + r), srcs[r], start=(r == 0), stop=(r == 3))
        evac(dst, ps.r(0, n))

    def load_kv_ext_fused(i, kvf, hh, slot, tiles):
        blk_ = hh // 4
        hh = hh % 4
        kvf = 512
        kT_all, v_all = W[i]["kT_all"][blk_], W[i]["v_all"][blk_]
        kT_loc, v_loc = W[i]["kT_loc"][blk_], W[i]["v_loc"][blk_]
        for s3 in range(3):
            es = [e for e in tiles if e // 8 == s3]
            if not es:
                continue
            e0, e1 = es[0], es[-1] + 1
            ta, tb = e0 % 8, (e1 - 1) % 8 + 1
            n = tb - ta
            t0, t1 = ta * 128, tb * 128
            if s3 == 1:
                if n == 1:
                    P.dma("sp", KE[slot].r(e0), dref(kT_loc.ap()[hh * 128:(hh + 1) * 128, t0:t1], "kTloc%d" % i))
                    P.dma("sp", VE[slot].r(e0), dref(v_loc.ap()[t0:t1, hh * 128:(hh + 1) * 128], "vloc%d" % i))
                else:
                    P.dma("sp", KE[slot].r(e0, a1=e1), dref(kT_loc.ap()[hh * 128:(hh + 1) * 128, t0:t1].rearrange("d (t k) -> d t k", k=128), "kTloc%d" % i))
                    P.dma("sp", VE[slot].r(e0, a1=e1), dref(v_loc.ap()[t0:t1, hh * 128:(hh + 1) * 128].rearrange("(t p) d -> p t d", p=128), "vloc%d" % i))
                continue
            rot_s = 3 if s3 == 0 else 1
            for r in range(4):
                P.dma("sp", STG.r(r, 0, n * 128), dref(kT_all.ap()[r * kvf + hh * 128:r * kvf + hh * 128 + 128, t0:t1], "kTall%d" % i))
            for c0 in range(0, n * 128, 512):
                wd_ = min(512, n * 128 - c0)
                sel(SELI_AB, rot_s, KEf[slot].r(0, e0 * 128 + c0, e0 * 128 + c0 + wd_), [STG.r(r, c0, c0 + wd_) for r in range(4)], wd_)
            for r in range(4):
                if n == 1:
                    P.dma("sp", STG4.r(r * 8), dref(v_all.ap()[r * T + t0:r * T + t1, hh * 128:(hh + 1) * 128], "vall%d" % i))
                else:
                    P.dma("sp", STG4.r(r * 8, a1=r * 8 + n), dref(v_all.ap()[r * T + t0:r * T + t1, hh * 128:(hh + 1) * 128].rearrange("(t p) d -> p t d", p=128), "vall%d" % i))
            for c0 in range(0, n * 128, 512):
                wd_ = min(512, n * 128 - c0)
                sel(SELI_AB, rot_s, VEf[slot].r(0, e0 * 128 + c0, e0 * 128 + c0 + wd_), [STG.r(r, c0, c0 + wd_) for r in range(4)], wd_)

    def load_kv_ext(i, kvf, hh, slot, tiles):
        if mode == "fused":
            return load_kv_ext_fused(i, kvf, hh, slot, tiles)
        kT_all, v_all = W[i]["kT_all"], W[i]["v_all"]
        for s3 in range(3):
            es = [e for e in tiles if e // 8 == s3]
            if not es:
                continue
            e0, e1 = es[0], es[-1] + 1
            sh = 3 + s3
            t0, t1 = (e0 % 8) * 128, ((e1 - 1) % 8 + 1) * 128
            if e1 - e0 == 1:
                srck = lambda e, sh=sh, t0=t0, t1=t1: kT_all.ap()[(sh - 3) * kvf + hh * 128:(sh - 3) * kvf + hh * 128 + 128, t0:t1]
                P.dma("sp", KE[slot].r(e0), dref(srck, "kTall%d" % i))
                srcv = lambda e, sh=sh, t0=t0: v_all.ap()[(sh - 3) * T + t0:(sh - 3) * T + t0 + 128, hh * 128:(hh + 1) * 128]
                P.dma("sp", VE[slot].r(e0), dref(srcv, "vall%d" % i))
                continue
            srck = lambda e, sh=sh, t0=t0, t1=t1: kT_all.ap()[(sh - 3) * kvf + hh * 128:(sh - 3) * kvf + hh * 128 + 128, t0:t1].rearrange("d (t k) -> d t k", k=128)
            P.dma("sp", KE[slot].r(e0, a1=e1), dref(srck, "kTall%d" % i))
            srcv = lambda e, sh=sh, t0=t0, t1=t1: v_all.ap()[(sh - 3) * T + t0:(sh - 3) * T + t1, hh * 128:(hh + 1) * 128].rearrange("(t p) d -> p t d", p=128)
            P.dma("sp", VE[slot].r(e0, a1=e1), dref(srcv, "vall%d" % i))

    def band_attention(i, m):
        sl = slopes16()
        kvf = 2048 if m == 1 else 512
        if m == 0:
            P.dma("sp", TAB.r(0, a1=3), dref(tabA.rearrange("p (o k) -> p o k", k=128), "tabA"))
            P.dma("sp", MISC.r(0, 0, 16), dref(W[i]["sink"], "sink"))
            P.act(MISC.r(0, 0, 16), MISC.r(0, 0, 16), AF.Exp)
        else:
            P.dma("sp", TAB.r(0, a1=25), dref(tabB.rearrange("p (o k) -> p o k", k=128), "tabB"))
        if mode == "fused":
            P.dma("sp", SELI_AB.r(0, a1=16), dref(seli.rearrange("p (o k) -> p o k", k=128), "seli"))
        nkv = 4 if m == 0 else 16
        def tiles_for(g):
            offs = [-1, 0, 1] if m == 0 else B_OFFS[g]
            return sorted(set(8 + qt + o for qt in range(8) for o in offs)), offs
        g0 = 0
        tl, _ = tiles_for(0)
        load_kv_ext(i, kvf, 0, 0, tl)
        first_in_group = {}
        cnt = 0
        nkv = min(nkv, DEBUG_SKIP.get('nkv', nkv))
        for kv in range(nkv):
            slot = kv % 2
            if kv + 1 < nkv:
                gn = 0 if m == 0 else b_group(kv + 1)
                tln, _ = tiles_for(gn)
                load_kv_ext(i, kvf, kv + 1, (kv + 1) % 2, tln)
            heads = [kv * 4 + j for j in range(4)] if m == 0 else [kv]
            for h in heads:
                g = 0 if m == 0 else b_group(h)
                offs = [-1, 0, 1] if m == 0 else B_OFFS[g]
                tb = 0 if m == 0 else B_TABBASE[g]
                for qg in range(2):
                    psO = PSB[1 + (cnt % 2)]
                    psD = PSB[3 + (cnt % 2)]
                    cnt += 1
                    pendb = []

                    def flushB(item, psO=psO, psD=psD, slot=slot):
                        c0_, c1_, e_, pt_, st_, sp_ = item
                        P.mm(psO.r(c0_, c1_), VE[slot].r(e_), pt_, start=st_, stop=sp_)
                        P.mm(psD.r(c0_, c1_), ONES.r(), pt_, start=st_, stop=sp_)
                    for qi in range(4):
                        qt = qg * 4 + qi
                        c0, c1 = qi * 128, qi * 128 + 128
                        for oi, o in enumerate(offs):
                            e = 8 + qt + o
                            k = rr.get("s", 0)
                            rr["s"] = (k + 1) % 8
                            psS = PSB[5 + k // 4 % 2] if False else PSB[5 + (k // 4)]
                            sc0 = (k % 4) * 128
                            P.mm(psS.r(sc0, sc0 + 128), KE[slot].r(e), QT.r(h, qt * 128, qt * 128 + 128))
                            tmp = TMP.r(k % 4)
                            P.stt(tmp, TAB.r(tb + oi), -sl[h], psS.r(sc0, sc0 + 128), ALU.mult, ALU.add)
                            pt = PT.r(k % 4)
                            P.act(pt, tmp, AF.Exp, bias=VB(e // 8))
                            pendb.append((c0, c1, e, pt, oi == 0, oi == len(offs) - 1))
                            if len(pendb) > 2:
                                flushB(pendb.pop(0))
                    for item in pendb:
                        flushB(item)
                    q0, q1 = qg * 512, qg * 512 + 512
                    rd = FIN.r(0)
                    if m == 0:
                        P.ts(rd, psD.r(), MISC.r(0, h, h + 1), None, ALU.add)
                        P.recip(rd, rd)
                        P.tt(HT.r(h, q0, q1), psO.r(), rd, ALU.mult)
                    else:
                        P.recip(rd, psD.r())
                        P.tt(HT.r(h, q0, q1), psO.r(), rd, ALU.mult)
                        ln = FIN.r(1)
                        if DEBUG_SKIP.get('noln'):
                            pass
                        elif (g, qg) not in first_in_group:
                            first_in_group[(g, qg)] = 1
                            P.act(SC.r(g * 2 + qg), rd, AF.Ln)
                        else:
                            P.act(ln, rd, AF.Ln)
                            P.tt(SC.r(g * 2 + qg), SC.r(g * 2 + qg), ln, ALU.add)
        if m == 1 and not DEBUG_SKIP.get('noalpha'):
            hg = [6.0, 5.0, 5.0]
            for qg in range(2):
                for g in range(3):
                    P.act(SC.r(g * 2 + qg), SC.r(g * 2 + qg), AF.Exp, scale=-1.0 / hg[g])
                tot = FIN.r(0)
                P.tt(tot, SC.r(0 + qg), SC.r(2 + qg), ALU.add)
                P.tt(tot, tot, SC.r(4 + qg), ALU.add)
                P.recip(tot, tot)
                for g in range(3):
                    P.tt(SC.r(g * 2 + qg), SC.r(g * 2 + qg), tot, ALU.mult)
                for h in range(nkv):
                    g = b_group(h)
                    P.tt(HT.r(h, qg * 512, qg * 512 + 512), HT.r(h, qg * 512, qg * 512 + 512), SC.r(g * 2 + qg), ALU.mult)

    def diff_attention(i, l):
        sl = slopes16()
        lam_init = 0.8 - 0.6 * math.exp(-0.3 * l)
        kT_all, v_all = W[i]["kT_all"], W[i]["v_all"]
        P.dma("sp", TABC.r(0, a1=5), dref(tabC.rearrange("p (o k) -> p o k", k=512), "tabC"))
        P.dma("sp", CB.r(), dref(cbias, "cbias"))
        P.dma("sp", CS.r(), dref(csig, "csig"))
        P.dma("sp", LAMT.r(), dref(W[i]["lam"], "lam"))
        P.dma("sp", LAMP.r(0, 8, 9), dref(W[i]["subg"], "subg"))
        P.tt(LAMT.r(0, 0, 64), LAMT.r(0, 0, 64), LAMT.r(0, 64, 128), ALU.mult)
        P.tt(LAMT.r(0, 128, 192), LAMT.r(0, 128, 192), LAMT.r(0, 192, 256), ALU.mult)
        P.add("dve", lambda e: e.reduce_sum(LAMP.r(0, 0, 1).ap, LAMT.r(0, 0, 64).ap, mybir.AxisListType.X), [LAMT.r(0, 0, 64)], [LAMP.r(0, 0, 1)])
        P.add("dve", lambda e: e.reduce_sum(LAMP.r(0, 1, 2).ap, LAMT.r(0, 128, 192).ap, mybir.AxisListType.X), [LAMT.r(0, 128, 192)], [LAMP.r(0, 1, 2)])
        P.act(LAMP.r(0, 0, 2), LAMP.r(0, 0, 2), AF.Exp)
        P.tt(LAMP.r(0, 2, 3), LAMP.r(0, 1, 2), LAMP.r(0, 0, 1), ALU.subtract)
        P.ts(LAMP.r(0, 2, 3), LAMP.r(0, 2, 3), -lam_init, None, ALU.add)
        P.ts(LAMP.r(0, 9, 10), LAMP.r(0, 8, 9), (1.0 - lam_init), None, ALU.mult)
        NLAM = LAMP.r(0, 2, 3)
        SG = LAMP.r(0, 9, 10)
        if mode == "fused":
            P.dma("sp", SELI_C.r(0, a1=16), dref(seli.rearrange("p (o k) -> p o k", k=128), "seli"))
        for kvh in range(4):
            if mode == "fused":
                kT_loc, v_loc = W[i]["kT_loc"][0], W[i]["v_loc"][0]
                kT_all, v_all = W[i]["kT_all"][0], W[i]["v_all"][0]
                P.dma("sp", KC.r(0, 0, 1024), dref(kT_loc.ap()[kvh * 128:(kvh + 1) * 128, :], "kTloc%d" % i))
                P.dma("sp", VC.r(0, a1=8), dref(v_loc.ap()[:, kvh * 128:(kvh + 1) * 128].rearrange("(t p) d -> p t d", p=128), "vloc%d" % i))
                for r in range(4):
                    P.dma("sp", STG.r(r), dref(kT_all.ap()[r * 512 + kvh * 128:r * 512 + kvh * 128 + 128, :], "kTall%d" % i))
                for s_ in (1, 2, 3):
                    for c0 in (0, 512):
                        sel(SELI_C, s_, KC.r(0, s_ * 1024 + c0, s_ * 1024 + c0 + 512), [STG.r(r, c0, c0 + 512) for r in range(4)], 512)
                for r in range(4):
                    P.dma("sp", STG4.r(r * 8, a1=r * 8 + 8), dref(v_all.ap()[r * T:(r + 1) * T, kvh * 128:(kvh + 1) * 128].rearrange("(t p) d -> p t d", p=128), "vall%d" % i))
                for s_ in (1, 2, 3):
                    for c0 in (0, 512):
                        sel(SELI_C, s_, VCf.r(0, s_ * 1024 + c0, s_ * 1024 + c0 + 512), [STG.r(r, c0, c0 + 512) for r in range(4)], 512)
            for s in (range(4) if mode != "fused" else ()):
                srck = lambda e, s=s, kvh=kvh: kT_all.ap()[s * 512 + kvh * 128:s * 512 + kvh * 128 + 128, :]
                P.dma("sp", KC.r(0, s * 1024, s * 1024 + 1024), dref(srck, "kTall%d" % i))
                srcv = lambda e, s=s, kvh=kvh: v_all.ap()[s * T:(s + 1) * T, kvh * 128:(kvh + 1) * 128].rearrange("(t p) d -> p t d", p=128)
                P.dma("sp", VC.r(s * 8, a1=s * 8 + 8), dref(srcv, "vall%d" % i))
            for hj in range(4):
                h = kvh * 4 + hj
                P.ts(MJ.r(), TABC.r(0), sl[h], None, ALU.mult)
                for qb in range(2):
                    q0, q1 = qb * 512, qb * 512 + 512
                    psO = [PSB[1], PSB[2]]
                    psD = [PSB[3], PSB[4]]
                    pend = []

                    def flushC(item):
                        c_, kt_, pt_ = item
                        P.mm(psO[c_].r(), VC.r(kt_), pt_, start=(kt_ == 0), stop=(kt_ == 31))
                        P.mm(psD[c_].r(), ONES.r(), pt_, start=(kt_ == 0), stop=(kt_ == 31))
                    for c in range(2):
                        p0, p1 = c * 64, c * 64 + 64
                        for kt in range(32):
                            s, t = kt // 8, kt % 8
                            k = rr.get("cs", 0)
                            rr["cs"] = (k + 1) % 3
                            psS = PSB[5 + k]
                            P.mm(psS.r(), KC.r(0, kt * 128, kt * 128 + 128, p0=p0, p1=p1), QT.r(h, q0, q1, p0=p0, p1=p1))
                            tmp = CTMP.r(k)
                            pt = CPT.r(k)
                            diag = (s == 0) and (qb * 4 <= t < qb * 4 + 4)
                            col = kt * 2 + qb
                            if diag:
                                P.stt(tmp, TABC.r(1 + (t - qb * 4)), -sl[h], psS.r(), ALU.mult, ALU.add)
                                P.act(pt, tmp, AF.Exp)
                            else:
                                P.stt(tmp, MJ.r(), CS.r(0, col, col + 1), psS.r(), ALU.mult, ALU.add)
                                P.act(pt, tmp, AF.Exp, bias=CB.r(0, h * 64 + col, h * 64 + col + 1))
                            pend.append((c, kt, pt))
                            if len(pend) > 2:
                                flushC(pend.pop(0))
                    for item in pend:
                        flushC(item)
                    r1, r2, o1, o2 = CF.r(0), CF.r(1), CF.r(2), CF.r(3)
                    P.recip(r1, psD[0].r())
                    P.tt(o1, psO[0].r(), r1, ALU.mult)
                    P.recip(r2, psD[1].r())
                    P.tt(o2, psO[1].r(), r2, ALU.mult)
                    P.stt(o1, o2, NLAM, o1, ALU.mult, ALU.add)
                    P.act(CSQ.r(), o1, AF.Square)
                    P.mm(psN.r(), ONES.r(), CSQ.r())
                    P.act(r1, psN.r(), AF.Sqrt, bias=EPS, scale=1.0 / 128.0)
                    P.recip(r1, r1)
                    P.stt(HT.r(h, q0, q1), o1, SG, r1, ALU.mult, ALU.mult)

    def wo_phase(i):
        w = W[i]["wo"]
        load_wblock(WOS[0], w, 0)
        for blk in range(4):
            if blk + 1 < 4:
                load_wblock(WOS[(blk + 1) % 2], w, (blk + 1) * 512)
            ws = WOS[blk % 2]
            for j in range(4):
                oc = blk * 4 + j
                for th in range(2):
                    ps = nextps()
                    for h in range(16):
                        P.mm(ps.r(), ws.r(h, j * 128, j * 128 + 128), HT.r(h, th * 512, th * 512 + 512), start=(h == 0), stop=(h == 15))
                    evac(YACC.r(oc, th * 512, th * 512 + 512), ps.r())

    def ffn_phase(i):
        wg, wu, wd = W[i]["wg"], W[i]["wu"], W[i]["wd"]

        def load_gu(hc):
            s = hc % 2
            for k0 in (0, 8):
                for wsrc, dst in ((wg, WGU[s][0]), (wu, WGU[s][1])):
                    src = wsrc[k0 * 128:(k0 + 8) * 128, hc * 128:(hc + 1) * 128].rearrange("(kc p) n -> p kc n", p=128)
                    P.dma("pool", dst.r(k0, a1=k0 + 8), dref(src, "w"))

        def load_d(hc):
            P.dma("pool", WD[hc % 4].r(), dref(wd[hc * 128:(hc + 1) * 128, :], "w"))

        def gateup(hc):
            s = hc % 2
            for th in range(2):
                t0, t1 = th * 512, th * 512 + 512
                pg = PSB[1 + (hc * 2 + th) % 2]
                pu = PSB[3 + (hc * 2 + th) % 2]
                for kc in range(16):
                    P.mm(pg.r(), WGU[s][0].r(kc), HT.r(kc, t0, t1), start=(kc == 0), stop=(kc == 15))
                for kc in range(16):
                    P.mm(pu.r(), WGU[s][1].r(kc), HT.r(kc, t0, t1), start=(kc == 0), stop=(kc == 15))
                sg = FINF.r((hc * 2 + th) % 2)
                P.act(sg, pg.r(), AF.Silu)
                P.tt(HID.r(hc % 4, t0, t1), sg, pu.r(), ALU.mult)

        def down(p):
            h0, h1 = 2 * p, 2 * p + 1
            for oc in range(16):
                for th in range(2):
                    t0, t1 = th * 512, th * 512 + 512
                    py = PSB[5 + (oc * 2 + th) % 3]
                    P.mm(py.r(), WD[h0 % 4].r(0, oc * 128, oc * 128 + 128), HID.r(h0 % 4, t0, t1), start=True, stop=False)
                    P.mm(py.r(), WD[h1 % 4].r(0, oc * 128, oc * 128 + 128), HID.r(h1 % 4, t0, t1), start=False, stop=True)
                    if p == 0:
                        evac(YACC.r(oc, t0, t1), py.r())
                    else:
                        P.tt(YACC.r(oc, t0, t1), YACC.r(oc, t0, t1), py.r(), ALU.add)

        npass = NHC // 2
        load_gu(0)
        load_gu(1)
        load_d(0)
        load_d(1)
        gateup(0)
        gateup(1)
        for p in range(npass):
            if p + 1 < npass:
                for hc in (2 * p + 2, 2 * p + 3):
                    load_gu(hc)
                    load_d(hc)
                    gateup(hc)
            down(p)

    FINF = SB(nc, "FINF", SO + 1536, 2, 512, F32, base)

    toks = []
    for i, l in enumerate(LS):
        m = l % 3
        if mode != "post":
            norm_pre(i * 64 + 0)
            qkv_phase(i, m)
        if mode == "pre":
            for kc in range(0, 16, 4):
                toks.append(P.dma("sp", dref(qout[kc:kc + 4].rearrange("k p t -> p k t"), "qout"), QT.r(kc, a1=kc + 4)))
            continue
        if mode == "post":
            for kc in range(0, 16, 4):
                P.dma("sp", QT.r(kc, a1=kc + 4), dref(qin[kc:kc + 4].rearrange("k p t -> p k t"), "qin"))
        if DEBUG_SKIP.get("att"):
            pass
        elif m == 2:
            diff_attention(i, l)
        else:
            band_attention(i, m)
        wo_phase(i)
        post_norm_res(i * 64 + 16)
        if not DEBUG_SKIP.get("ffn"):
            norm_pre(i * 64 + 32)
            ffn_phase(i)
            post_norm_res(i * 64 + 48)

    if mode == "pre":
        for sid, n in list(P.dma_cnt.items()):
            if sid[1] == "sp":
                toks.append((sid, n))
    else:
        for kc in range(0, 16, 4):
            toks.append(P.dma("sp", dref(xout[kc:kc + 4].rearrange("k p t -> p k t"), "xout"), XT.r(kc, a1=kc + 4)))
    P.emit(toks)
    return nc


LAYER_GROUPS = [[0], [1], [2], [3]]
DEBUG_SKIP = {}
N_LAYERS = 4
_cache = {}


def _tables():
    ii = np.arange(128)[:, None].astype(np.float64)
    jj = np.arange(128)[None, :].astype(np.float64)
    tabA = np.zeros((128, 3, 128), np.float32)
    for oi, o in enumerate((-1, 0, 1)):
        d = o * 128 + ii - jj
        tabA[:, oi, :] = np.where(np.abs(d) <= 128, np.abs(d), BIG)
    tabB = np.zeros((128, 25, 128), np.float32)
    for g in range(3):
        dil = B_DIL[g]
        for oi, o in enumerate(B_OFFS[g]):
            d = o * 128 + ii - jj
            ok = (np.abs(d) <= 64 * dil) & (np.mod(d, dil) == 0)
            tabB[:, B_TABBASE[g] + oi, :] = np.where(ok, np.abs(d), BIG)
    j5 = np.arange(512)[None, :].astype(np.float64)
    tabC = np.zeros((128, 5, 512), np.float32)
    tabC[:, 0, :] = np.broadcast_to(j5, (128, 512))
    for c in range(4):
        tabC[:, 1 + c, :] = np.abs(ii - j5 + 128 * c)
    return tabA.reshape(128, -1), tabB.reshape(128, -1), tabC.reshape(128, -1)


def _c_tables(q):
    sl = slopes16()
    ii = np.arange(128).astype(np.float64)
    csig = np.zeros((128, 64), np.float32)
    cbias = np.zeros((128, 16, 64), np.float32)
    for s in range(4):
        for t in range(8):
            for qb in range(2):
                col = (s * 8 + t) * 2 + qb
                kabs0 = ((q + s) % 4) * 1024 + t * 128
                qabs0 = q * 1024 + qb * 512
                Dk = kabs0 - qabs0
                if Dk >= 512:
                    csig[:, col] = 1.0
                    for h in range(16):
                        cbias[:, h, col] = -sl[h] * (Dk + ii)
                elif Dk <= -128:
                    csig[:, col] = -1.0
                    for h in range(16):
                        cbias[:, h, col] = sl[h] * (Dk + ii)
    return csig, cbias.reshape(128, 1024)


def _get_nc(l, mode):
    key = (l % 3, mode)
    if key not in _cache:
        _cache[key] = build([l], mode)
    return _cache[key]


FUSED = True


def _kernel_fused(x, mix_pre_g, mix_post_g, ffn_pre_g, ffn_post_g, w_gate, w_up, w_down,
                  a_w_qkv, a_w_o, a_sink, b_w_qkv, b_w_o, c_w_qkv, c_w_o,
                  c_lambda_q1, c_lambda_k1, c_lambda_q2, c_lambda_k2, c_subln_g):
    f = lambda a: np.ascontiguousarray(np.asarray(a), dtype=np.float32)
    x = f(x)
    gl = [f(mix_pre_g), f(mix_post_g), f(ffn_pre_g), f(ffn_post_g)]
    tabA, tabB, tabC = _tables()
    LS = list(range(N_LAYERS))
    key = ("fused", tuple(LS))
    if key not in _cache:
        _cache[key] = build(LS, "fused")
    nc = _cache[key]
    shared = {"ones": np.ones((128, 128), np.float32)}
    has = set(l % 3 for l in LS)
    if 0 in has:
        shared["tabA"] = tabA
    if 1 in has:
        shared["tabB"] = tabB
    if 2 in has:
        shared["tabC"] = tabC
    gains = np.zeros((128, len(LS) * 64), np.float32)
    for i, l in enumerate(LS):
        m, j = l % 3, l // 3
        for n in range(4):
            gains[:, i * 64 + n * 16:i * 64 + n * 16 + 16] = gl[n][l].reshape(16, 128).T
        if not DEBUG_SKIP.get("ffn"):
            shared["wg%d" % i] = f(w_gate[l])
            shared["wu%d" % i] = f(w_up[l])
            shared["wd%d" % i] = f(w_down[l])
        shared["wqkv%d" % i] = f((a_w_qkv, b_w_qkv, c_w_qkv)[m][j])
        shared["wo%d" % i] = f((a_w_o, b_w_o, c_w_o)[m][j])
        if m == 0:
            shared["sink%d" % i] = np.ascontiguousarray(np.broadcast_to(f(a_sink[j])[None, :], (128, 16)))
        elif m == 2:
            lam = np.concatenate([f(c_lambda_q1[j]), f(c_lambda_k1[j]), f(c_lambda_q2[j]), f(c_lambda_k2[j])])
            shared["lam%d" % i] = np.ascontiguousarray(np.broadcast_to(lam[None, :], (128, 256)))
            shared["subg%d" % i] = np.ascontiguousarray(f(c_subln_g[j]).reshape(128, 1))
    shared["gains"] = gains
    in_maps = []
    eye = np.eye(128, dtype=np.float32)
    for c in range(8):
        b, q = c // 4, c % 4
        d = dict(shared)
        d["xT"] = np.ascontiguousarray(x[b, q * 1024:(q + 1) * 1024, :].T).reshape(16, 128, 1024)
        vb = np.zeros((128, 4), np.float32)
        if q == 0:
            vb[:, 0] = NEGV
        if q == 3:
            vb[:, 2] = NEGV
        d["vb"] = vb
        se = np.zeros((128, 16, 128), np.float32)
        for s_ in range(4):
            se[:, s_ * 4 + (q + s_) % 4, :] = eye
        d["seli"] = se.reshape(128, 2048).astype(ml_dtypes.bfloat16)
        if 2 in has:
            cs, cb = _c_tables(q)
            d["csig"] = cs
            d["cbias"] = cb
        in_maps.append(d)
    res = run_bass_kernel_spmd(nc, in_maps, core_ids=list(range(8)))
    out = np.zeros((2, 4096, 2048), np.float32)
    for c in range(8):
        b, q = c // 4, c % 4
        out[b, q * 1024:(q + 1) * 1024, :] = np.asarray(res.results[c]["outT"]).reshape(2048, 1024).T
    return out


def kernel(x, mix_pre_g, mix_post_g, ffn_pre_g, ffn_post_g, w_gate, w_up, w_down,
           a_w_qkv, a_w_o, a_sink, b_w_qkv, b_w_o, c_w_qkv, c_w_o,
           c_lambda_q1, c_lambda_k1, c_lambda_q2, c_lambda_k2, c_subln_g):
    if FUSED:
        return _kernel_fused(x, mix_pre_g, mix_post_g, ffn_pre_g, ffn_post_g, w_gate, w_up, w_down,
                             a_w_qkv, a_w_o, a_sink, b_w_qkv, b_w_o, c_w_qkv, c_w_o,
                             c_lambda_q1, c_lambda_k1, c_lambda_q2, c_lambda_k2, c_subln_g)
    f = lambda a: np.ascontiguousarray(np.asarray(a), dtype=np.float32)
    x = f(x)
    gl = [f(mix_pre_g), f(mix_post_g), f(ffn_pre_g), f(ffn_post_g)]
    tabA, tabB, tabC = _tables()
    ones = np.ones((128, 128), np.float32)
    xT = []
    for c in range(8):
        b, q = c // 4, c % 4
        xT.append(np.ascontiguousarray(x[b, q * 1024:(q + 1) * 1024, :].T).reshape(16, 128, 1024))
    for l in range(N_LAYERS):
        m, j = l % 3, l // 3
        gains = np.zeros((128, 64), np.float32)
        for n in range(4):
            gains[:, n * 16:n * 16 + 16] = gl[n][l].reshape(16, 128).T
        wq = f((a_w_qkv, b_w_qkv, c_w_qkv)[m][j])
        wo = f((a_w_o, b_w_o, c_w_o)[m][j])
        common = {"ones": ones, "gains": gains}
        nc = _get_nc(l, "pre")
        in_maps = []
        for c in range(8):
            d = dict(common)
            d["xT"] = xT[c]
            d["vb"] = np.zeros((128, 4), np.float32)
            d["wqkv0"] = wq
            in_maps.append(d)
        res = run_bass_kernel_spmd(nc, in_maps, core_ids=list(range(8)))
        qT = [np.asarray(res.results[c]["qTout"]) for c in range(8)]
        kT = [np.asarray(res.results[c]["kTloc"]) for c in range(8)]
        vv = [np.asarray(res.results[c]["vloc"]) for c in range(8)]
        nc = _get_nc(l, "post")
        in_maps = []
        for c in range(8):
            b, q = c // 4, c % 4
            d = dict(common)
            d["xT"] = xT[c]
            d["qTin"] = qT[c]
            if m == 2:
                qs = [(q + s_) % 4 for s_ in range(4)]
            else:
                qs = [(q + 3) % 4, q, (q + 1) % 4]
            d["kText"] = np.ascontiguousarray(np.concatenate([kT[b * 4 + r] for r in qs], axis=0))
            d["vext"] = np.ascontiguousarray(np.concatenate([vv[b * 4 + r] for r in qs], axis=0))
            vb = np.zeros((128, 4), np.float32)
            if q == 0:
                vb[:, 0] = NEGV
            if q == 3:
                vb[:, 2] = NEGV
            d["vb"] = vb
            d["wo0"] = wo
            d["wg0"] = f(w_gate[l])
            d["wu0"] = f(w_up[l])
            d["wd0"] = f(w_down[l])
            if m == 0:
                d["tabA"] = tabA
                d["sink0"] = np.ascontiguousarray(np.broadcast_to(f(a_sink[j])[None, :], (128, 16)))
            elif m == 1:
                d["tabB"] = tabB
            else:
                d["tabC"] = tabC
                lam = np.concatenate([f(c_lambda_q1[j]), f(c_lambda_k1[j]), f(c_lambda_q2[j]), f(c_lambda_k2[j])])
                d["lam0"] = np.ascontiguousarray(np.broadcast_to(lam[None, :], (128, 256)))
                d["subg0"] = np.ascontiguousarray(f(c_subln_g[j]).reshape(128, 1))
                cs, cb = _c_tables(q)
                d["csig"] = cs
                d["cbias"] = cb
            in_maps.append(d)
        res = run_bass_kernel_spmd(nc, in_maps, core_ids=list(range(8)))
        xT = [np.asarray(res.results[c]["outT"]) for c in range(8)]
    out = np.zeros((2, 4096, 2048), np.float32)
    for c in range(8):
        b, q = c // 4, c % 4
        out[b, q * 1024:(q + 1) * 1024, :] = xT[c].reshape(2048, 1024).T
    return out
```
